# Optimizing a Trainium2 kernel written in Bass

```python
import math
import jax, jax.numpy as jnp
from jax import lax
import numpy as np

D_MODEL = 4096
BATCH = 4
SEQ = 4096
DEPTH = 1

D_FF = 11008
D_SSM = D_MODEL // 2
SSM_GROUP = 16
N_SSM_GROUPS = D_SSM // SSM_GROUP
SSM_STATE = 64
DT_MIN = 0.001
DT_MAX = 0.1
D_CONV = D_MODEL // 2
CONV_WIDTH = 3
N_MEM = 256
N_XHEADS = 4
XHEAD_DIM = D_MODEL // N_XHEADS
MIX_IN_COLS = D_SSM + 3 * D_CONV + 2 * D_MODEL
MIX_SPLITS = (D_SSM, D_SSM + D_CONV, D_SSM + 2 * D_CONV, D_SSM + 3 * D_CONV,
              D_SSM + 3 * D_CONV + D_MODEL)
RMS_EPS = 1e-6

kernel_name = "hybrid_s5_shortconv_gated_macaron_xattn"


def rms_norm(x, g):
    xf = x.astype(jnp.float32)
    y = xf * lax.rsqrt(jnp.mean(xf * xf, axis=-1, keepdims=True) + RMS_EPS)
    return (y * g.astype(jnp.float32)).astype(x.dtype)


def swiglu(h, w_in, w_out):
    a, b = jnp.split(h @ w_in, 2, axis=-1)
    return (jax.nn.silu(a) * b) @ w_out


def _ssm_combine(e1, e2):
    a1, b1 = e1
    a2, b2 = e2
    return a1 * a2, a2 * b1 + b2


def s5_mixer(u, a_re, a_im, log_dt, b_re, b_im, c_re, c_im, d_skip):
    bsz, seq, _ = u.shape
    f32 = jnp.float32
    uf = u.astype(f32).reshape(bsz, seq, N_SSM_GROUPS, SSM_GROUP)
    lam = lax.complex(a_re.astype(f32), a_im.astype(f32))
    dt = jnp.exp(log_dt.astype(f32))[:, None]
    lam_bar = jnp.exp(lam * dt)
    b = lax.complex(b_re.astype(f32), b_im.astype(f32))
    b_bar = ((lam_bar - 1.0) / lam)[..., None] * b
    c = lax.complex(c_re.astype(f32), c_im.astype(f32))
    bu = jnp.einsum('blgh,gph->blgp', uf, b_bar)
    a_elems = jnp.broadcast_to(lam_bar, (1, seq) + lam_bar.shape)
    _, states = lax.associative_scan(_ssm_combine, (a_elems, bu), axis=1)
    y = jnp.einsum('blgp,ghp->blgh', states, c).real
    y = y.reshape(bsz, seq, D_SSM) + d_skip.astype(f32) * uf.reshape(bsz, seq, D_SSM)
    return y.astype(u.dtype)


def causal_depthwise_conv(z, w):
    k, c = w.shape
    rhs = w.astype(z.dtype)[:, None, :]
    return lax.conv_general_dilated(z, rhs, window_strides=(1,), padding=((k - 1, 0),),
                                    dimension_numbers=('NWC', 'WIO', 'NWC'),
                                    feature_group_count=c)


def cross_attention(hq, mkv, wq, wk, wv, wo):
    bsz, seq, _ = hq.shape
    n_mem = mkv.shape[1]
    q = (hq @ wq).reshape(bsz, seq, N_XHEADS, XHEAD_DIM)
    k = (mkv @ wk).reshape(bsz, n_mem, N_XHEADS, XHEAD_DIM)
    v = (mkv @ wv).reshape(bsz, n_mem, N_XHEADS, XHEAD_DIM)
    s = jnp.einsum('blhd,bmhd->bhlm', q, k).astype(jnp.float32) * (XHEAD_DIM ** -0.5)
    p = jax.nn.softmax(s, axis=-1).astype(v.dtype)
    o = jnp.einsum('bhlm,bmhd->blhd', p, v).reshape(bsz, seq, D_MODEL)
    return o @ wo


def setup_inputs(seed: int = 0) -> dict:
    key = jax.random.key(seed)
    ks = iter(jax.random.split(key, 40))
    f32 = jnp.float32

    def nrm(shape, scale):
        return jax.random.normal(next(ks), shape, f32) * scale

    def gain(shape):
        return 1.0 + 0.02 * jax.random.normal(next(ks), shape, f32)

    L, G, P, H = DEPTH, N_SSM_GROUPS, SSM_STATE, SSM_GROUP
    a_im_base = math.pi * jnp.arange(P, dtype=f32)
    return {
        "x": nrm((BATCH, SEQ, D_MODEL), 1.0),
        "mem": nrm((BATCH, N_MEM, D_MODEL), 1.0),
        "ffn1_norm": gain((L, D_MODEL)),
        "ffn1_w_in": nrm((L, D_MODEL, 2 * D_FF), D_MODEL ** -0.5),
        "ffn1_w_out": nrm((L, D_FF, D_MODEL), D_FF ** -0.5),
        "mix_norm": gain((L, D_MODEL)),
        "mix_w_in": nrm((L, D_MODEL, MIX_IN_COLS), D_MODEL ** -0.5),
        "ssm_a_re": -0.5 + nrm((L, G, P), 0.01),
        "ssm_a_im": a_im_base + nrm((L, G, P), 0.01),
        "ssm_log_dt": jax.random.uniform(next(ks), (L, G), f32,
                                         math.log(DT_MIN), math.log(DT_MAX)),
        "ssm_b_re": nrm((L, G, P, H), (2 * H) ** -0.5),
        "ssm_b_im": nrm((L, G, P, H), (2 * H) ** -0.5),
        "ssm_c_re": nrm((L, G, H, P), 1.0),
        "ssm_c_im": nrm((L, G, H, P), 1.0),
        "ssm_d": nrm((L, D_SSM), 1.0),
        "ssm_glu_w": nrm((L, D_SSM, 2 * D_MODEL), D_SSM ** -0.5),
        "conv_w": nrm((L, CONV_WIDTH, D_CONV), CONV_WIDTH ** -0.5),
        "conv_w_out": nrm((L, D_CONV, D_MODEL), D_CONV ** -0.5),
        "mix_w_out": nrm((L, D_MODEL, D_MODEL), D_MODEL ** -0.5),
        "xattn_norm": gain((L, D_MODEL)),
        "mem_norm": gain((L, D_MODEL)),
        "xattn_wq": nrm((L, D_MODEL, D_MODEL), D_MODEL ** -0.5),
        "xattn_wk": nrm((L, D_MODEL, D_MODEL), D_MODEL ** -0.5),
        "xattn_wv": nrm((L, D_MODEL, D_MODEL), D_MODEL ** -0.5),
        "xattn_wo": nrm((L, D_MODEL, D_MODEL), D_MODEL ** -0.5),
        "ffn2_norm": gain((L, D_MODEL)),
        "ffn2_w_in": nrm((L, D_MODEL, 2 * D_FF), D_MODEL ** -0.5),
        "ffn2_w_out": nrm((L, D_FF, D_MODEL), D_FF ** -0.5),
        "final_norm": gain((D_MODEL,)),
    }


def reference(x, mem, ffn1_norm, ffn1_w_in, ffn1_w_out, mix_norm, mix_w_in,
              ssm_a_re, ssm_a_im, ssm_log_dt, ssm_b_re, ssm_b_im, ssm_c_re, ssm_c_im,
              ssm_d, ssm_glu_w, conv_w, conv_w_out, mix_w_out,
              xattn_norm, mem_norm, xattn_wq, xattn_wk, xattn_wv, xattn_wo,
              ffn2_norm, ffn2_w_in, ffn2_w_out, final_norm):
    h = x
    for l in range(DEPTH):
        h = h + 0.5 * swiglu(rms_norm(h, ffn1_norm[l]), ffn1_w_in[l], ffn1_w_out[l])

        u = rms_norm(h, mix_norm[l])
        u_ssm, cb, cc, ch, g_a, g_b = jnp.split(u @ mix_w_in[l], MIX_SPLITS, axis=-1)

        ys = jax.nn.gelu(s5_mixer(u_ssm, ssm_a_re[l], ssm_a_im[l], ssm_log_dt[l],
                                  ssm_b_re[l], ssm_b_im[l], ssm_c_re[l], ssm_c_im[l],
                                  ssm_d[l]), approximate=False)
        val, gl = jnp.split(ys @ ssm_glu_w[l], 2, axis=-1)
        y_a = val * jax.nn.sigmoid(gl)

        y_b = (cb * causal_depthwise_conv(cc * ch, conv_w[l])) @ conv_w_out[l]

        merged = jax.nn.sigmoid(g_a) * y_a + jax.nn.sigmoid(g_b) * y_b
        h = h + merged @ mix_w_out[l]

        h = h + cross_attention(rms_norm(h, xattn_norm[l]), rms_norm(mem, mem_norm[l]),
                                xattn_wq[l], xattn_wk[l], xattn_wv[l], xattn_wo[l])

        h = h + 0.5 * swiglu(rms_norm(h, ffn2_norm[l]), ffn2_w_in[l], ffn2_w_out[l])
    return rms_norm(h, final_norm)
```

```python
import math
from contextlib import ExitStack
import numpy as np
import concourse.bass as bass
import concourse.mybir as mybir
from concourse.bass_utils import run_bass_kernel_spmd

F32 = mybir.dt.float32
BF16 = mybir.dt.bfloat16
I32 = mybir.dt.int32
ALU = mybir.AluOpType
AF = mybir.ActivationFunctionType
AX = mybir.AxisListType

D = 4096
KC = 32
T = 512
DFF = 11008
HC = 86
PARTS = [22, 22, 21, 21]
NMEM = 256
NPAIR = 64
EPS = 1e-6
TWO_PI = 2.0 * math.pi
NWB = 4
NHK = 4
NHN = 3
PF = 3

CFG = {"nt_pre": 4, "nt_main": 4}


class Buf:
    __slots__ = ("name", "lw", "rd", "rd_dma")

    def __init__(self, name):
        self.name = name
        self.lw = None
        self.rd = {}
        self.rd_dma = []


class Op:
    __slots__ = ("eng", "fn", "deps", "dkey", "needs_inc", "val", "idx", "big")


ENGS = ("pe", "act", "dve", "pool", "sp")


class Prog:
    def __init__(self):
        self.ops = {e: [] for e in ENGS}
        self.dcount = {}
        self.nbuf = 0

    def buf(self, name="b"):
        self.nbuf += 1
        return Buf(name)

    def bufs(self, n, name="b"):
        return [self.buf(f"{name}{i}") for i in range(n)]

    def op(self, eng, fn, r=(), w=(), dkey=None, big=False):
        o = Op()
        o.big = big
        o.eng = eng
        o.fn = fn
        o.dkey = dkey
        o.needs_inc = False
        o.val = None
        o.idx = len(self.ops[eng])
        deps = {}

        def add(d):
            if d is None or d is o:
                return
            if d.dkey is not None:
                deps[("d", id(d))] = d
            else:
                if d.eng == "pe" and eng == "pe":
                    return
                if d.eng == "dve" and eng == "dve" and big and d.big:
                    return
                k = ("c", d.eng)
                if k not in deps or deps[k].idx < d.idx:
                    deps[k] = d

        for b in r:
            add(b.lw)
        for b in w:
            add(b.lw)
            for d in b.rd.values():
                add(d)
            for d in b.rd_dma:
                add(d)
        o.deps = list(deps.values())
        for d in o.deps:
            d.needs_inc = True
        if dkey is not None:
            self.dcount[dkey] = self.dcount.get(dkey, 0) + 16
            o.val = self.dcount[dkey]
        for b in r:
            if dkey is not None:
                b.rd_dma.append(o)
            else:
                b.rd[eng] = o
        for b in w:
            b.lw = o
            b.rd = {}
            b.rd_dma = []
        self.ops[eng].append(o)
        return o

    def emit(self, nc, es):
        csem = {e: es.enter_context(nc.semaphore("s_" + e)) for e in ("pe", "act", "dve", "pool")}
        dsem = {k: es.enter_context(nc.semaphore("d_" + k)) for k in self.dcount}
        for e in ("pe", "act", "dve", "pool"):
            c = 0
            for o in self.ops[e]:
                if o.needs_inc:
                    c += 1
                    o.val = c
        block = es.enter_context(nc.Block())
        prog = self

        def run(engname, eng):
            waited = {}
            for o in prog.ops[engname]:
                for d in o.deps:
                    if d.dkey is not None:
                        sem = dsem[d.dkey]
                        v = prog.dcount[d.dkey] if d.dkey == "init" else d.val
                        key = "d_" + d.dkey
                    else:
                        sem = csem[d.eng]
                        v = d.val
                        key = d.eng
                    if waited.get(key, 0) >= v:
                        continue
                    eng.wait_ge(sem, v)
                    waited[key] = v
                ins = o.fn(eng)
                if o.dkey is not None:
                    ins.then_inc(dsem[o.dkey], 16)
                elif o.needs_inc:
                    ins.then_inc(csem[engname], 1)
            if engname == "sp":
                for k, tot in prog.dcount.items():
                    eng.wait_ge(dsem[k], tot)

        @block.tensor
        def _(e):
            run("pe", e)

        @block.scalar
        def _(e):
            run("act", e)

        @block.vector
        def _(e):
            run("dve", e)

        @block.gpsimd
        def _(e):
            run("pool", e)

        @block.sync
        def _(e):
            run("sp", e)


def build(cfg):
    wseq = _build(cfg, None)
    return _build(cfg, wseq)


AHEAD = 3


def _build(cfg, WSEQ):
    dry = WSEQ is None
    REC = []
    nt_pre, nt_main = cfg["nt_pre"], cfg["nt_main"]
    nc = bass.Bass("TRN2", target_bir_lowering=False)
    P = Prog()
    es = ExitStack()

    def din(name, shape):
        return nc.dram_tensor(name, list(shape), F32, kind="ExternalInput").ap()

    x_main = din("x_main", [max(nt_main, 1), KC, 128, T])
    x_pre = din("x_pre", [max(nt_pre, 1), KC, 128, T])
    memT = din("memT", [KC, 128, NMEM])
    maskb = din("maskb", [128, 1])
    ident_d = din("ident", [128, 128])
    gains_d = din("gains", [128, 7, KC])
    WSCR = {}
    WAP = {}

    def dinw(name, shape):
        ap = din(name, shape)
        WAP[name] = ap
        WSCR[name] = nc.dram_tensor("scr_" + name, list(shape), BF16, kind="Internal").ap()
        return (name, ap)

    w_f1in = dinw("w_f1in", [2 * HC, 128, KC * 128])
    w_f1out = [dinw(f"w_f1out{i}", [KC, 128, PARTS[i] * 128]) for i in range(4)]
    w_f2in = dinw("w_f2in", [2 * HC, 128, KC * 128])
    w_f2out = [dinw(f"w_f2out{i}", [KC, 128, PARTS[i] * 128]) for i in range(4)]
    w_mixin = dinw("w_mixin", [128, 128, KC * 128])
    w_glu = dinw("w_glu", [64, 128, 16 * 128])
    w_cvout = dinw("w_cvout", [KC, 128, 16 * 128])
    w_mixout = dinw("w_mixout", [KC, 128, KC * 128])
    w_q = dinw("w_q", [KC, 128, KC * 128])
    w_k = dinw("w_k", [KC, 128, KC * 128])
    w_v = dinw("w_v", [KC, 128, KC * 128])
    w_o = dinw("w_o", [KC, 128, KC * 128])
    ssm_lane = din("ssm_lane", [128, 3, NPAIR])
    ssm_w1 = din("ssm_w1", [128, 5, 16 * 128])
    ssm_w3 = din("ssm_w3", [128, 2, NPAIR * 32])
    ssm_d = din("ssm_d", [128, 16])
    conv_w = din("conv_w", [128, 3, 16])
    out_d = nc.dram_tensor("out", [max(nt_main, 1), KC, 128, T], F32, kind="ExternalOutput").ap()
    hbuf = nc.dram_tensor("hbuf", [KC, 128, T], F32, kind="Internal").ap()
    tab_d = nc.dram_tensor("tab", [NPAIR, 128, 2 * T], F32, kind="Internal").ap()

    def sb(name, shape, dt=F32):
        return es.enter_context(nc.sbuf_tensor(name, list(shape), dt))

    xn = sb("xn", [128, KC, T], BF16)
    ar_a = sb("ar_a", [128, KC, T], BF16)
    ar_b = sb("ar_b", [128, 16, T], BF16)
    ar_c = sb("ar_c", [128, 16, T], BF16)
    tmp = sb("tmp", [128, 6, T], F32)
    srb = sb("srb", [128, 2, T], BF16)
    wst = [sb(f"wst{i}", [128, KC * 64], F32) for i in range(2)]
    wbf = [sb(f"wbf{i}", [128, KC * 128], BF16) for i in range(NWB)]
    hk = [sb(f"hk{i}", [128, T], F32) for i in range(NHK)]
    hn = [sb(f"hn{i}", [128, T], F32) for i in range(NHN)]
    sq = sb("sq", [128, T], F32)
    rstd = sb("rstd", [128, T], F32)
    tb = [sb(f"tb{i}", [128, 2 * T], F32) for i in range(2)]
    w1re = sb("w1re", [128, 16, 128], BF16)
    w1im = sb("w1im", [128, 16, 128], BF16)
    w3re = sb("w3re", [128, NPAIR, 32], BF16)
    w3im = sb("w3im", [128, NPAIR, 32], BF16)
    gains = sb("gains_s", [128, 7, KC])
    ones = sb("ones", [128, 128])
    ident = sb("ident_s", [128, 128])
    epst = sb("epst", [128, 1])
    mb = sb("mb", [128, 1])
    dsk = sb("dsk", [128, 16])
    cw = sb("cw", [128, 3, 16])
    S = sb("S", [128, NPAIR, 2])
    Rl = sb("Rl", [128, NPAIR])
    thl = sb("thl", [128, NPAIR])
    lane_in = sb("lane_in", [128, 3, NPAIR])
    halo = sb("halo", [128, 16, 2])
    cch = sb("cch", [128, T + 2])
    io = sq
    sm = sb("sm", [128, 8])
    psum = [es.enter_context(nc.psum_tensor(f"ps{i}", [128, T], F32)) for i in range(8)]

    B_xn = P.bufs(KC, "xn")
    B_a = P.bufs(KC, "ara")
    B_b = P.bufs(16, "arb")
    B_c = P.bufs(16, "arc")
    B_tmp = P.bufs(6, "tmp")
    B_srb = P.bufs(2, "srb")
    B_wst = P.bufs(2, "wst")
    B_wbf = P.bufs(NWB, "wbf")
    B_hk = P.bufs(NHK, "hk")
    B_hn = P.bufs(NHN, "hn")
    B_sq = P.buf("sq")
    B_rstd = P.buf("rstd")
    B_tbh = P.bufs(4, "tbh")
    B_tb = [[B_tbh[0], B_tbh[1]], [B_tbh[2], B_tbh[3]]]
    B_ps = P.bufs(8, "ps")
    B_const = P.buf("const")
    B_S = P.buf("S")
    B_halo = P.buf("halo")
    B_cch = P.buf("cch")
    B_sm = P.buf("sm")
    B_hbuf = P.bufs(KC, "hbuf")
    B_tab = P.bufs(NPAIR, "tab")
    B_w1 = P.buf("w1")
    B_w3 = P.buf("w3")

    st = {"w": 0, "hkw": 0, "issued": 0, "ws": 0, "ps": 0, "hk": 0, "hn": 0, "cast": 0, "tb": 0}

    def next_ps():
        i = st["ps"] % 4
        st["ps"] += 1
        return i

    B_init = P.buf("initscratch")
    B_pc = P.buf("poolconst")
    B_lds = []

    def ld_const(dst_ap, src_ap):
        b = P.buf("ld")
        B_lds.append(b)
        P.op("sp", lambda e, d=dst_ap, s=src_ap: e.dma_start(out=d, in_=s), w=[b], dkey="init")

    ld_const(gains[:], gains_d[:, :, :])
    ld_const(ident[:], ident_d[:, :])
    ld_const(mb[:], maskb[:, :])
    ld_const(dsk[:], ssm_d[:, :])
    ld_const(cw[:], conv_w[:, :, :])
    ld_const(lane_in[:], ssm_lane[:, :, :])
    wa = ar_a[:].rearrange("p a b -> p (a b)").bitcast(F32).rearrange("p (a b) -> p a b", a=4)
    wb_ = xn[:].rearrange("p a b -> p (a b)").bitcast(F32).rearrange("p (a b) -> p a b", a=4)
    wc = ar_b[:].rearrange("p a b -> p (a b)").bitcast(F32).rearrange("p (a b) -> p a b", a=2)
    wd = ar_c[:].rearrange("p a b -> p (a b)").bitcast(F32).rearrange("p (a b) -> p a b", a=2)
    for j in range(5):
        dst = wa[:, j, :] if j < 4 else wb_[:, 0, :]
        ld_const(dst, ssm_w1[:, j, :])
    P.op("pool", lambda e: e.memset(ones[:], 1.0), w=[B_pc])
    P.op("pool", lambda e: e.memset(epst[:], EPS), w=[B_pc])
    P.op("pool", lambda e: e.iota(io[:], pattern=[[1, T]], base=1, channel_multiplier=0,
                                  allow_small_or_imprecise_dtypes=True), w=[B_pc])
    P.op("pool", lambda e: e.memset(S[:], 0.0), w=[B_S])
    P.op("pool", lambda e: e.memset(halo[:], 0.0), w=[B_halo])

    IR = [B_init, B_pc] + B_lds

    def dve(fn, r=(), w=(), big=False):
        return P.op("dve", fn, r=list(r), w=list(w), big=big)

    def act(fn, r=(), w=()):
        return P.op("act", fn, r=list(r), w=list(w))

    def idve(fn, extra_w=()):
        return P.op("dve", fn, r=IR, w=[B_init] + list(extra_w))

    def iact(fn, extra_w=()):
        return P.op("act", fn, r=IR, w=[B_init] + list(extra_w))

    def range_reduce(x_ap, k_i32, k_f32):
        idve(lambda e: e.tensor_scalar(k_i32, x_ap, 1.0 / TWO_PI, None, ALU.mult))
        idve(lambda e: e.tensor_copy(k_f32, k_i32))
        idve(lambda e: e.scalar_tensor_tensor(x_ap, k_f32, -TWO_PI, x_ap, ALU.mult, ALU.add))
        idve(lambda e: e.tensor_scalar(k_f32, x_ap, math.pi, None, ALU.is_gt))
        idve(lambda e: e.scalar_tensor_tensor(x_ap, k_f32, -TWO_PI, x_ap, ALU.mult, ALU.add))
        idve(lambda e: e.tensor_scalar(k_f32, x_ap, -math.pi, None, ALU.is_lt))
        idve(lambda e: e.scalar_tensor_tensor(x_ap, k_f32, TWO_PI, x_ap, ALU.mult, ALU.add))

    dtl = tmp[:, 0, 0:NPAIR]
    ki_l = tmp[:, 1, 0:NPAIR].bitcast(I32)
    kf_l = tmp[:, 2, 0:NPAIR]
    iact(lambda e: e.activation(dtl, lane_in[:, 2, :], AF.Exp))
    idve(lambda e: e.tensor_tensor(Rl[:], lane_in[:, 0, :], dtl, ALU.mult))
    iact(lambda e: e.activation(Rl[:], Rl[:], AF.Exp))
    idve(lambda e: e.tensor_tensor(thl[:], lane_in[:, 1, :], dtl, ALU.mult))
    range_reduce(thl[:], ki_l, kf_l)

    for pi in range(NPAIR):
        slot = pi % 2
        tbs, Bt = tb[slot], B_tb[slot]
        ang = tmp[:, 0, :]
        ki = tmp[:, 1, :].bitcast(I32)
        kf = tmp[:, 2, :]
        ang2 = tmp[:, 3, :]
        idve(lambda e, pi=pi, ang=ang: e.tensor_scalar(ang, io[:], thl[:, pi:pi + 1], None, ALU.mult))
        idve(lambda e, ang=ang, ang2=ang2: e.tensor_scalar(ang2, ang, math.pi / 2, None, ALU.add))
        range_reduce(ang, ki, kf)
        iact(lambda e, tbs=tbs, ang=ang: e.activation(tbs[:, T:2 * T], ang, AF.Sin), extra_w=Bt)
        range_reduce(ang2, ki, kf)
        iact(lambda e, tbs=tbs, ang2=ang2: e.activation(tbs[:, 0:T], ang2, AF.Sin), extra_w=Bt)
        P.op("sp", lambda e, pi=pi, tbs=tbs: e.dma_start(out=tab_d[pi, :, :], in_=tbs[:]), r=Bt, w=[B_tab[pi]],
             dkey=f"tb{slot}")

    a_re, a_im, ldt, b_re, b_im = wa[:, 0, :], wa[:, 1, :], wa[:, 2, :], wa[:, 3, :], wb_[:, 0, :]
    t1, t2, t3 = wb_[:, 1, :], wb_[:, 2, :], wb_[:, 3, :]
    t4, t5, t6, t7 = wc[:, 0, :], wc[:, 1, :], wd[:, 0, :], wd[:, 1, :]
    iact(lambda e: e.activation(ldt, ldt, AF.Exp))
    idve(lambda e: e.tensor_tensor(t1, a_re, ldt, ALU.mult))
    iact(lambda e: e.activation(t1, t1, AF.Exp))
    idve(lambda e: e.tensor_tensor(t2, a_im, ldt, ALU.mult))
    idve(lambda e: e.tensor_scalar(t3, t2, math.pi / 2, None, ALU.add))
    range_reduce(t2, t6.bitcast(I32), t7)
    range_reduce(t3, t6.bitcast(I32), t7)
    iact(lambda e: e.activation(t2, t2, AF.Sin))
    iact(lambda e: e.activation(t3, t3, AF.Sin))
    idve(lambda e: e.tensor_tensor(t3, t3, t1, ALU.mult))
    idve(lambda e: e.tensor_tensor(t2, t2, t1, ALU.mult))
    idve(lambda e: e.tensor_scalar(t3, t3, -1.0, None, ALU.add))
    idve(lambda e: e.tensor_tensor(t1, a_re, a_re, ALU.mult))
    idve(lambda e: e.tensor_tensor(t4, a_im, a_im, ALU.mult))
    idve(lambda e: e.tensor_tensor(t1, t1, t4, ALU.add))
    idve(lambda e: e.reciprocal(t1, t1))
    idve(lambda e: e.tensor_tensor(t4, t3, a_re, ALU.mult))
    idve(lambda e: e.tensor_tensor(t5, t2, a_im, ALU.mult))
    idve(lambda e: e.tensor_tensor(t4, t4, t5, ALU.add))
    idve(lambda e: e.tensor_tensor(t4, t4, t1, ALU.mult))
    idve(lambda e: e.tensor_tensor(t5, t2, a_re, ALU.mult))
    idve(lambda e: e.tensor_tensor(t6, t3, a_im, ALU.mult))
    idve(lambda e: e.tensor_tensor(t5, t5, t6, ALU.subtract))
    idve(lambda e: e.tensor_tensor(t5, t5, t1, ALU.mult))
    idve(lambda e: e.tensor_tensor(t6, t4, b_re, ALU.mult))
    idve(lambda e: e.tensor_tensor(t7, t5, b_im, ALU.mult))
    idve(lambda e: e.tensor_tensor(w1re[:].rearrange("p a b -> p (a b)"), t6, t7, ALU.subtract), extra_w=[B_w1])
    idve(lambda e: e.tensor_tensor(t6, t4, b_im, ALU.mult))
    idve(lambda e: e.tensor_tensor(t7, t5, b_re, ALU.mult))
    idve(lambda e: e.tensor_tensor(w1im[:].rearrange("p a b -> p (a b)"), t6, t7, ALU.add), extra_w=[B_w1])
    P.op("sp", lambda e: e.dma_start(out=t6, in_=ssm_w3[:, 0, :]), r=[B_init], w=[B_init], dkey="i2a")
    P.op("sp", lambda e: e.dma_start(out=t7, in_=ssm_w3[:, 1, :]), r=[B_init], w=[B_init], dkey="i2b")
    idve(lambda e: e.tensor_copy(w3re[:].rearrange("p a b -> p (a b)"), t6), extra_w=[B_w3])
    idve(lambda e: e.tensor_scalar(w3im[:].rearrange("p a b -> p (a b)"), t7, -1.0, None, ALU.mult), extra_w=[B_w3])
    idve(lambda e: e.memset(sm[:, 7:8], 0.0), extra_w=[B_const, B_sq] + B_a + B_xn + B_b + B_c + B_tmp)

    SEEN = {}
    PEND = []

    def flush_store():
        i, n, scr, blk, b = PEND.pop(0)
        P.op("sp", lambda e, i=i, n=n, scr=scr, blk=blk: e.dma_start(out=scr[blk, :, :], in_=wbf[i][:, 0:n]),
             r=[B_wbf[i]], w=[b], dkey=f"wb{i}")

    def issue_load(k):
        wref, blk, ncols = WSEQ[k]
        wname = wref[0]
        wap = WAP[wname]
        key = (wname, blk)
        i = k % NWB
        n = ncols
        while len(PEND) > 2 or any(p[0] == i for p in PEND):
            flush_store()
        if key not in SEEN:
            nh = n // 2
            for j in range(2):
                P.op("sp", lambda e, j=j, nh=nh, wap=wap, blk=blk: e.dma_start(out=wst[j][:, 0:nh],
                                                                              in_=wap[blk, :, j * nh:(j + 1) * nh]),
                     w=[B_wst[j]], dkey=f"ws{j}")
                ce = "dve" if (st["cast"] % 3 == 2) else "act"
                st["cast"] += 1
                if ce == "act":
                    P.op("act", lambda e, i=i, j=j, nh=nh: e.activation(wbf[i][:, j * nh:(j + 1) * nh], wst[j][:, 0:nh], AF.Copy),
                         r=[B_wst[j]], w=[B_wbf[i]])
                else:
                    P.op("dve", lambda e, i=i, j=j, nh=nh: e.tensor_copy(wbf[i][:, j * nh:(j + 1) * nh], wst[j][:, 0:nh]),
                         r=[B_wst[j]], w=[B_wbf[i]])
            b = P.buf("scr")
            SEEN[key] = b
            PEND.append((i, n, WSCR[wname], blk, b))
        else:
            for pe_ in list(PEND):
                if pe_[4] is SEEN[key]:
                    while PEND:
                        flush_store()
            scr = WSCR[wname]
            P.op("sp", lambda e, i=i, n=n, scr=scr, blk=blk: e.dma_start(out=wbf[i][:, 0:n], in_=scr[blk, :, :]),
                 r=[SEEN[key]], w=[B_wbf[i]], dkey=f"wl{i}")

    def wload(wref, blk, ncols):
        k = st["w"]
        st["w"] += 1
        if dry:
            REC.append((wref, blk, ncols))
            return wbf[k % NWB], B_wbf[k % NWB]
        assert WSEQ[k][1] == blk and WSEQ[k][2] == ncols and WSEQ[k][0][0] == wref[0]
        while st["issued"] < min(k + AHEAD + 1, len(WSEQ)):
            issue_load(st["issued"])
            st["issued"] += 1
        return wbf[k % NWB], B_wbf[k % NWB]

    def mm_group(ps_i, wt, Bw, rhs_list, n=T, ps_cols=None):
        nk = len(rhs_list)
        pcols = ps_cols if ps_cols is not None else slice(0, n)
        for k, (rap, rb) in enumerate(rhs_list):
            P.op("pe", lambda e, k=k, rap=rap, ps_i=ps_i, wt=wt, nk=nk, pcols=pcols: e.matmul(
                psum[ps_i][:, pcols], wt[:, k * 128:(k + 1) * 128], rap, start=(k == 0), stop=(k == nk - 1)),
                r=[Bw, rb], w=[B_ps[ps_i]])

    tbh = [tb[0][:, 0:T], tb[0][:, T:2 * T], tb[1][:, 0:T], tb[1][:, T:2 * T]]

    def hk_load(src_ap, n=T, extra_r=(), wide=False):
        if wide:
            i = st["hkw"] % (NHK + 4)
            st["hkw"] += 1
        else:
            i = st["hk"] % NHK
            st["hk"] += 1
        if i < NHK:
            buf, Bb, key = hk[i], B_hk[i], f"hk{i}"
        else:
            buf, Bb, key = tbh[i - NHK], B_tbh[i - NHK], f"th{i - NHK}"
        P.op("sp", lambda e, buf=buf, s=src_ap, n=n: e.dma_start(out=buf[:, 0:n], in_=s), r=list(extra_r),
             w=[Bb], dkey=key)
        return buf, Bb

    def rms_stats(chunk_src, n, nchunks=KC):
        ps_i = next_ps()
        for k in range(nchunks):
            src, rb = chunk_src(k)
            hkt, Bh = hk_load(src, n, rb, wide=True)
            act(lambda e, hkt=hkt, n=n: e.activation(sq[:, 0:n], hkt[:, 0:n], AF.Square), r=[Bh], w=[B_sq])
            P.op("pe", lambda e, k=k, n=n, ps_i=ps_i, nchunks=nchunks: e.matmul(
                psum[ps_i][:, 0:n], ones[:, :], sq[:, 0:n], start=(k == 0), stop=(k == nchunks - 1)),
                r=[B_sq, B_const], w=[B_ps[ps_i]])
        act(lambda e, n=n, ps_i=ps_i: e.activation(rstd[:, 0:n], psum[ps_i][:, 0:n], AF.Sqrt, bias=epst[:, 0:1],
                                                     scale=1.0 / D), r=[B_ps[ps_i], B_const], w=[B_rstd])
        dve(lambda e, n=n: e.reciprocal(rstd[:, 0:n], rstd[:, 0:n]), r=[B_rstd], w=[B_rstd])

    def norm_from_hbuf(gi):
        rms_stats(lambda k: (hbuf[k, :, :], [B_hbuf[k]]), T)
        for k in range(KC):
            hkt, Bh = hk_load(hbuf[k, :, :], T, [B_hbuf[k]], wide=True)
            dve(lambda e, k=k, hkt=hkt: e.scalar_tensor_tensor(xn[:, k, :], hkt[:, :], gains[:, gi, k:k + 1], rstd[:, :],
                                                               ALU.mult, ALU.mult),
                r=[Bh, B_rstd, B_const], w=[B_xn[k]])

    HKQ = {}

    def ep_prefetch(m):
        if m < KC:
            HKQ[m] = hk_load(hbuf[m, :, :], T, [B_hbuf[m]])

    def ep_begin():
        for m in range(PF):
            ep_prefetch(m)

    def epilogue(ps_i, m, scale):
        ep_prefetch(m + PF)
        hkt, Bh = HKQ.pop(m)
        j = st["hn"] % NHN
        st["hn"] += 1
        dve(lambda e, hkt=hkt, j=j, ps_i=ps_i, scale=scale: e.scalar_tensor_tensor(
            hn[j][:, :], psum[ps_i][:, :], scale, hkt[:, :], ALU.mult, ALU.add),
            r=[B_ps[ps_i], Bh], w=[B_hn[j]])
        P.op("sp", lambda e, j=j, m=m: e.dma_start(out=hbuf[m, :, :], in_=hn[j][:, :]), r=[B_hn[j]], w=[B_hbuf[m]],
             dkey=f"hn{j}")

    xn_rhs = [(xn[:, k, :], B_xn[k]) for k in range(KC)]

    def ffn(w_in, w_out, gi):
        norm_from_hbuf(gi)
        hc0 = 0
        for part, npart in enumerate(PARTS):
            for jj in range(npart):
                j = hc0 + jj
                wa_t, Bwa = wload(w_in, j, KC * 128)
                pa = next_ps()
                mm_group(pa, wa_t, Bwa, xn_rhs)
                wb_t, Bwb = wload(w_in, HC + j, KC * 128)
                pb = next_ps()
                mm_group(pb, wb_t, Bwb, xn_rhs)
                ti = jj % 2
                act(lambda e, pa=pa, ti=ti: e.activation(tmp[:, ti, :], psum[pa][:, :], AF.Silu),
                    r=[B_ps[pa]], w=[B_tmp[ti]])
                dve(lambda e, pb=pb, ti=ti, jj=jj: e.tensor_tensor(ar_a[:, jj, :], tmp[:, ti, :], psum[pb][:, :], ALU.mult),
                    r=[B_ps[pb], B_tmp[ti]], w=[B_a[jj]])
            g_rhs = [(ar_a[:, jj, :], B_a[jj]) for jj in range(npart)]
            ep_begin()
            for m in range(KC):
                wo_t, Bwo = wload(w_out[part], m, npart * 128)
                po = next_ps()
                mm_group(po, wo_t, Bwo, g_rhs)
                epilogue(po, m, 0.5)
            hc0 += npart

    def ssm(full, B_u):
        for c in range(16):
            yps = next_ps() if full else None
            for q in range(4):
                pi = 4 * c + q
                par = pi % 2
                pA, pB = 4 + 2 * par, 5 + 2 * par
                rows = slice(32 * q, 32 * q + 32)
                P.op("pe", lambda e, c=c, q=q, rows=rows, pA=pA: e.matmul(
                    psum[pA][:, :], w1re[rows, c, :], ar_a[rows, c, :], start=True, stop=True, tile_position=(32 * q, 0)),
                    r=[B_w1, B_u[c]], w=[B_ps[pA]])
                P.op("pe", lambda e, c=c, q=q, rows=rows, pB=pB: e.matmul(
                    psum[pB][:, :], w1im[rows, c, :], ar_a[rows, c, :], start=True, stop=True, tile_position=(32 * q, 0)),
                    r=[B_w1, B_u[c]], w=[B_ps[pB]])
                ts = st["tb"] % 2
                st["tb"] += 1
                P.op("sp", lambda e, ts=ts, pi=pi: e.dma_start(out=tb[ts][:], in_=tab_d[pi, :, :]), r=[B_tab[pi]],
                     w=B_tb[ts], dkey=f"tb{ts}")
                cs, sn = tb[ts][:, 0:T], tb[ts][:, T:2 * T]
                Bt = B_tb[ts]
                A_, B_ = psum[pA][:, :], psum[pB][:, :]
                t = [tmp[:, i, :] for i in range(6)]
                TT = lambda o, a, b, op, r, w, big=full: dve(lambda e, o=o, a=a, b=b, op=op: e.tensor_tensor(o, a, b, op), r=r, w=w, big=big)
                TT(t[0], A_, cs, ALU.mult, [B_ps[pA]] + Bt, [B_tmp[0]], big=True)
                TT(t[1], B_, sn, ALU.mult, [B_ps[pB]] + Bt, [B_tmp[1]], big=True)
                TT(t[0], t[0], t[1], ALU.add, [B_tmp[0], B_tmp[1]], [B_tmp[0]], big=True)
                TT(t[2], B_, cs, ALU.mult, [B_ps[pB]] + Bt, [B_tmp[2]], big=True)
                TT(t[3], A_, sn, ALU.mult, [B_ps[pA]] + Bt, [B_tmp[3]], big=True)
                TT(t[2], t[2], t[3], ALU.subtract, [B_tmp[2], B_tmp[3]], [B_tmp[2]], big=True)
                rbc = Rl[:, pi:pi + 1].to_broadcast([128, T])
                dve(lambda e, pi=pi, rbc=rbc, o=t[1], i_=t[0]: e.tensor_tensor_scan(o, rbc, i_, S[:, pi, 0:1], ALU.mult, ALU.add),
                    r=[B_tmp[0], B_S, B_init], w=[B_tmp[1]], big=True)
                dve(lambda e, pi=pi, rbc=rbc, o=t[3], i_=t[2]: e.tensor_tensor_scan(o, rbc, i_, S[:, pi, 1:2], ALU.mult, ALU.add),
                    r=[B_tmp[2], B_S, B_init], w=[B_tmp[3]], big=True)
                n0 = 0 if full else T - 1
                sl = slice(n0, T)
                TT(t[0][:, sl], tb[ts][:, n0:T], t[1][:, sl], ALU.mult, [B_tmp[1]] + Bt, [B_tmp[0]])
                TT(t[2][:, sl], tb[ts][:, T + n0:2 * T], t[3][:, sl], ALU.mult, [B_tmp[3]] + Bt, [B_tmp[2]])
                TT(t[4][:, sl], t[0][:, sl], t[2][:, sl], ALU.subtract, [B_tmp[0], B_tmp[2]], [B_tmp[4]])
                TT(t[0][:, sl], tb[ts][:, T + n0:2 * T], t[1][:, sl], ALU.mult, [B_tmp[1]] + Bt, [B_tmp[0]])
                TT(t[2][:, sl], tb[ts][:, n0:T], t[3][:, sl], ALU.mult, [B_tmp[3]] + Bt, [B_tmp[2]])
                TT(t[5][:, sl], t[0][:, sl], t[2][:, sl], ALU.add, [B_tmp[0], B_tmp[2]], [B_tmp[5]])
                dve(lambda e, pi=pi: e.tensor_copy(S[:, pi, 0:1], tmp[:, 4, T - 1:T]), r=[B_tmp[4]], w=[B_S])
                dve(lambda e, pi=pi: e.tensor_copy(S[:, pi, 1:2], tmp[:, 5, T - 1:T]), r=[B_tmp[5]], w=[B_S])
                if full:
                    act(lambda e: e.activation(srb[:, 0, :], tmp[:, 4, :], AF.Copy), r=[B_tmp[4]], w=[B_srb[0]])
                    act(lambda e: e.activation(srb[:, 1, :], tmp[:, 5, :], AF.Copy), r=[B_tmp[5]], w=[B_srb[1]])
                    P.op("pe", lambda e, pi=pi, rows=rows, yps=yps, q=q: e.matmul(
                        psum[yps][rows, :], w3re[:, pi, :], srb[:, 0, :], start=True, stop=False, tile_position=(0, 32 * q)),
                        r=[B_w3, B_srb[0]], w=[B_ps[yps]])
                    P.op("pe", lambda e, pi=pi, rows=rows, yps=yps, q=q: e.matmul(
                        psum[yps][rows, :], w3im[:, pi, :], srb[:, 1, :], start=False, stop=True, tile_position=(0, 32 * q)),
                        r=[B_w3, B_srb[1]], w=[B_ps[yps]])
            if full:
                dve(lambda e, c=c, yps=yps: e.scalar_tensor_tensor(tmp[:, 0, :], ar_a[:, c, :], dsk[:, c:c + 1], psum[yps][:, :],
                                                                   ALU.mult, ALU.add),
                    r=[B_u[c], B_ps[yps], B_const], w=[B_tmp[0]])
                act(lambda e, c=c: e.activation(ar_c[:, c, :], tmp[:, 0, :], AF.Gelu), r=[B_tmp[0]], w=[B_c[c]])

    def conv_branch(full):
        for c in range(16):
            wt, Bw = wload(w_mixin, 32 + c, KC * 128)
            pcc = next_ps()
            mm_group(pcc, wt, Bw, xn_rhs)
            act(lambda e, pcc=pcc: e.activation(tmp[:, 0, :], psum[pcc][:, :], AF.Copy), r=[B_ps[pcc]], w=[B_tmp[0]])
            wt, Bw = wload(w_mixin, 48 + c, KC * 128)
            pch = next_ps()
            mm_group(pch, wt, Bw, xn_rhs)
            dve(lambda e, c=c: e.tensor_copy(cch[:, 0:2], halo[:, c, :]), r=[B_halo], w=[B_cch])
            dve(lambda e, pch=pch: e.tensor_tensor(cch[:, 2:T + 2], tmp[:, 0, :], psum[pch][:, :], ALU.mult),
                r=[B_tmp[0], B_ps[pch]], w=[B_cch])
            dve(lambda e, c=c: e.tensor_copy(halo[:, c, :], cch[:, T:T + 2]), r=[B_cch], w=[B_halo])
            if not full:
                continue
            wt, Bw = wload(w_mixin, 16 + c, KC * 128)
            pcb = next_ps()
            mm_group(pcb, wt, Bw, xn_rhs)
            dve(lambda e, c=c: e.tensor_scalar(tmp[:, 1, :], cch[:, 0:T], cw[:, 0, c:c + 1], None, ALU.mult),
                r=[B_cch, B_const], w=[B_tmp[1]])
            dve(lambda e, c=c: e.scalar_tensor_tensor(tmp[:, 1, :], cch[:, 1:T + 1], cw[:, 1, c:c + 1], tmp[:, 1, :],
                                                      ALU.mult, ALU.add), r=[B_cch, B_const, B_tmp[1]], w=[B_tmp[1]])
            dve(lambda e, c=c: e.scalar_tensor_tensor(tmp[:, 1, :], cch[:, 2:T + 2], cw[:, 2, c:c + 1], tmp[:, 1, :],
                                                      ALU.mult, ALU.add), r=[B_cch, B_const, B_tmp[1]], w=[B_tmp[1]])
            dve(lambda e, c=c, pcb=pcb: e.tensor_tensor(ar_b[:, c, :], tmp[:, 1, :], psum[pcb][:, :], ALU.mult),
                r=[B_tmp[1], B_ps[pcb]], w=[B_b[c]])

    def u_proj():
        for c in range(16):
            wt, Bw = wload(w_mixin, c, KC * 128)
            pu = next_ps()
            mm_group(pu, wt, Bw, xn_rhs)
            act(lambda e, c=c, pu=pu: e.activation(ar_a[:, c, :], psum[pu][:, :], AF.Copy), r=[B_ps[pu]], w=[B_a[c]])

    def mixer():
        norm_from_hbuf(1)
        u_proj()
        conv_branch(True)
        ssm(True, B_a)
        ys_rhs = [(ar_c[:, c, :], B_c[c]) for c in range(16)]
        cv_rhs = [(ar_b[:, c, :], B_b[c]) for c in range(16)]
        for m in range(KC):
            wt, Bw = wload(w_mixin, 64 + m, KC * 128)
            p1 = next_ps()
            mm_group(p1, wt, Bw, xn_rhs)
            act(lambda e, p1=p1: e.activation(tmp[:, 0, :], psum[p1][:, :], AF.Sigmoid), r=[B_ps[p1]], w=[B_tmp[0]])
            wt, Bw = wload(w_glu, m, 16 * 128)
            p2 = next_ps()
            mm_group(p2, wt, Bw, ys_rhs)
            dve(lambda e, p2=p2: e.tensor_tensor(tmp[:, 1, :], tmp[:, 0, :], psum[p2][:, :], ALU.mult),
                r=[B_tmp[0], B_ps[p2]], w=[B_tmp[1]])
            wt, Bw = wload(w_glu, 32 + m, 16 * 128)
            p3 = next_ps()
            mm_group(p3, wt, Bw, ys_rhs)
            act(lambda e, p3=p3: e.activation(tmp[:, 2, :], psum[p3][:, :], AF.Sigmoid), r=[B_ps[p3]], w=[B_tmp[2]])
            dve(lambda e: e.tensor_tensor(tmp[:, 1, :], tmp[:, 1, :], tmp[:, 2, :], ALU.mult),
                r=[B_tmp[1], B_tmp[2]], w=[B_tmp[1]])
            wt, Bw = wload(w_mixin, 96 + m, KC * 128)
            p4 = next_ps()
            mm_group(p4, wt, Bw, xn_rhs)
            act(lambda e, p4=p4: e.activation(tmp[:, 3, :], psum[p4][:, :], AF.Sigmoid), r=[B_ps[p4]], w=[B_tmp[3]])
            wt, Bw = wload(w_cvout, m, 16 * 128)
            p5 = next_ps()
            mm_group(p5, wt, Bw, cv_rhs)
            dve(lambda e, p5=p5: e.tensor_tensor(tmp[:, 4, :], tmp[:, 3, :], psum[p5][:, :], ALU.mult),
                r=[B_tmp[3], B_ps[p5]], w=[B_tmp[4]])
            dve(lambda e, m=m: e.tensor_tensor(ar_a[:, m, :], tmp[:, 1, :], tmp[:, 4, :], ALU.add),
                r=[B_tmp[1], B_tmp[4]], w=[B_a[m]])
        mg_rhs = [(ar_a[:, k, :], B_a[k]) for k in range(KC)]
        ep_begin()
        for m in range(KC):
            wt, Bw = wload(w_mixout, m, KC * 128)
            po = next_ps()
            mm_group(po, wt, Bw, mg_rhs)
            epilogue(po, m, 1.0)

    def xattn():
        norm_from_hbuf(2)
        memn = ar_b[:].rearrange("p a b -> p (a b)").rearrange("p (a b) -> p a b", a=KC)
        rms_stats(lambda k: (memT[k, :, :], []), NMEM)
        for k in range(KC):
            hkt, Bh = hk_load(memT[k, :, :], NMEM, (), wide=True)
            dve(lambda e, k=k, hkt=hkt: e.scalar_tensor_tensor(memn[:, k, :], hkt[:, 0:NMEM], gains[:, 3, k:k + 1],
                                                               rstd[:, 0:NMEM], ALU.mult, ALU.mult),
                r=[Bh, B_rstd, B_const], w=[B_b[k // 2]])
        mem_rhs = [(memn[:, k, :], B_b[k // 2]) for k in range(KC)]
        arc_flat = ar_c[:].rearrange("p a b -> p (a b)")
        KTh = arc_flat[:, 0:2048].rearrange("p (a b) -> p a b", a=8)
        Vh = arc_flat[:, 2048:4096].rearrange("p (a b) -> p a b", a=2)
        qTh = arc_flat[:, 4096:8192].rearrange("p (a b) -> p a b", a=8)
        B_KT, B_V, B_q = B_c[0], B_c[1], B_c[2]
        for h in range(4):
            for dc in range(8):
                wt, Bw = wload(w_k, h * 8 + dc, KC * 128)
                pk = next_ps()
                mm_group(pk, wt, Bw, mem_rhs, n=NMEM)
                act(lambda e, dc=dc, pk=pk: e.activation(KTh[:, dc, :], psum[pk][:, 0:NMEM], AF.Copy), r=[B_ps[pk]], w=[B_KT])
            for dc in range(8):
                wt, Bw = wload(w_v, h * 8 + dc, KC * 128)
                pv = next_ps()
                for mc in range(2):
                    for k in range(KC):
                        P.op("pe", lambda e, k=k, mc=mc, pv=pv, wt=wt: e.matmul(
                            psum[pv][:, mc * 128:(mc + 1) * 128], memn[:, k, mc * 128:(mc + 1) * 128],
                            wt[:, k * 128:(k + 1) * 128], start=(k == 0), stop=(k == KC - 1)),
                            r=[Bw, B_b[k // 2]], w=[B_ps[pv]])
                for mc in range(2):
                    act(lambda e, dc=dc, mc=mc, pv=pv: e.activation(Vh[:, mc, dc * 128:(dc + 1) * 128],
                                                                     psum[pv][:, mc * 128:(mc + 1) * 128], AF.Copy),
                        r=[B_ps[pv]], w=[B_V])
            for dc in range(8):
                wt, Bw = wload(w_q, h * 8 + dc, KC * 128)
                pq = next_ps()
                mm_group(pq, wt, Bw, xn_rhs)
                act(lambda e, dc=dc, pq=pq: e.activation(qTh[:, dc, :], psum[pq][:, :], AF.Copy, scale=1.0 / 32.0),
                    r=[B_ps[pq]], w=[B_q])
            for tc in range(4):
                pss = next_ps()
                for dc in range(8):
                    P.op("pe", lambda e, dc=dc, tc=tc, pss=pss: e.matmul(
                        psum[pss][:, 0:NMEM], qTh[:, dc, tc * 128:(tc + 1) * 128], KTh[:, dc, :],
                        start=(dc == 0), stop=(dc == 7)), r=[B_q, B_KT], w=[B_ps[pss]])
                dve(lambda e, pss=pss: e.reduce_max(sm[:, 0:1], psum[pss][:, 0:NMEM], AX.X), r=[B_ps[pss]], w=[B_sm])
                dve(lambda e: e.tensor_scalar(sm[:, 1:2], sm[:, 0:1], -1.0, None, ALU.mult), r=[B_sm], w=[B_sm])
                act(lambda e, pss=pss: e.activation(tmp[:, 0, 0:NMEM], psum[pss][:, 0:NMEM], AF.Exp, bias=sm[:, 1:2],
                                                     accum_out=sm[:, 2:3]), r=[B_ps[pss], B_sm], w=[B_tmp[0], B_sm])
                dve(lambda e: e.reciprocal(sm[:, 3:4], sm[:, 2:3]), r=[B_sm], w=[B_sm])
                dve(lambda e: e.tensor_scalar(tmp[:, 1, 0:NMEM], tmp[:, 0, 0:NMEM], sm[:, 3:4], None, ALU.mult),
                    r=[B_tmp[0], B_sm], w=[B_tmp[1]])
                for mc in range(2):
                    ppt = next_ps()
                    P.op("pe", lambda e, mc=mc, ppt=ppt: e.matmul(psum[ppt][:, 0:128], tmp[:, 1, mc * 128:(mc + 1) * 128],
                                                                  ident[:, :], start=True, stop=True),
                         r=[B_tmp[1], B_const], w=[B_ps[ppt]])
                    act(lambda e, mc=mc, tc=tc, ppt=ppt: e.activation(srb[:, mc, tc * 128:(tc + 1) * 128], psum[ppt][:, 0:128],
                                                                       AF.Copy), r=[B_ps[ppt]], w=[B_srb[mc]])
            for dc in range(8):
                po_ = next_ps()
                for mc in range(2):
                    P.op("pe", lambda e, dc=dc, mc=mc, po_=po_: e.matmul(
                        psum[po_][:, :], Vh[:, mc, dc * 128:(dc + 1) * 128], srb[:, mc, :], start=(mc == 0), stop=(mc == 1)),
                        r=[B_V, B_srb[mc]], w=[B_ps[po_]])
                act(lambda e, dc=dc, h=h, po_=po_: e.activation(ar_a[:, h * 8 + dc, :], psum[po_][:, :], AF.Copy),
                    r=[B_ps[po_]], w=[B_a[h * 8 + dc]])
        o_rhs = [(ar_a[:, k, :], B_a[k]) for k in range(KC)]
        ep_begin()
        for m in range(KC):
            wt, Bw = wload(w_o, m, KC * 128)
            po = next_ps()
            mm_group(po, wt, Bw, o_rhs)
            epilogue(po, m, 1.0)

    def final_norm(ti):
        rms_stats(lambda k: (hbuf[k, :, :], [B_hbuf[k]]), T)
        for k in range(KC):
            hkt, Bh = hk_load(hbuf[k, :, :], T, [B_hbuf[k]], wide=True)
            j = st["hn"] % NHN
            st["hn"] += 1
            dve(lambda e, k=k, hkt=hkt, j=j: e.scalar_tensor_tensor(hn[j][:, :], hkt[:, :], gains[:, 5, k:k + 1], rstd[:, :],
                                                                    ALU.mult, ALU.mult),
                r=[Bh, B_rstd, B_const], w=[B_hn[j]])
            P.op("sp", lambda e, j=j, k=k, ti=ti: e.dma_start(out=out_d[ti, k, :, :], in_=hn[j][:, :]), r=[B_hn[j]],
                 dkey=f"hn{j}")

    def load_h(src):
        P.op("sp", lambda e, src=src: e.dma_start(out=hbuf[:, :, :], in_=src), w=B_hbuf, dkey="hcopy")

    for ti in range(nt_pre):
        load_h(x_pre[ti, :, :, :])
        ffn(w_f1in, w_f1out, 0)
        norm_from_hbuf(1)
        u_proj()
        if ti == nt_pre - 1:
            conv_branch(False)
        ssm(False, B_a)
    if nt_pre > 0:
        dve(lambda e: e.tensor_scalar(S[:].rearrange("p a b -> p (a b)"), S[:].rearrange("p a b -> p (a b)"), mb[:, 0:1],
                                      None, ALU.mult), r=[B_S, B_const], w=[B_S])
        dve(lambda e: e.tensor_scalar(halo[:].rearrange("p a b -> p (a b)"), halo[:].rearrange("p a b -> p (a b)"),
                                      mb[:, 0:1], None, ALU.mult), r=[B_halo, B_const], w=[B_halo])
    for ti in range(nt_main):
        load_h(x_main[ti, :, :, :])
        ffn(w_f1in, w_f1out, 0)
        mixer()
        xattn()
        ffn(w_f2in, w_f2out, 4)
        final_norm(ti)

    while PEND:
        flush_store()
    if dry:
        es.close()
        return REC
    P.emit(nc, es)
    es.close()
    return nc


def _blk(Wm, kparts=None):
    K, N = Wm.shape
    kc, nm = K // 128, N // 128
    return np.ascontiguousarray(Wm.reshape(kc, 128, nm, 128).transpose(2, 1, 0, 3).reshape(nm, 128, kc * 128))


def _fm(xtok):
    nt = xtok.shape[0] // T
    return np.ascontiguousarray(xtok.reshape(nt, T, KC, 128).transpose(0, 2, 3, 1))


_CACHE = {}


def kernel(x, mem, ffn1_norm, ffn1_w_in, ffn1_w_out, mix_norm, mix_w_in,
           ssm_a_re, ssm_a_im, ssm_log_dt, ssm_b_re, ssm_b_im, ssm_c_re, ssm_c_im,
           ssm_d, ssm_glu_w, conv_w, conv_w_out, mix_w_out,
           xattn_norm, mem_norm, xattn_wq, xattn_wk, xattn_wv, xattn_wo,
           ffn2_norm, ffn2_w_in, ffn2_w_out, final_norm, _cfg=None):
    cfg = dict(CFG if _cfg is None else _cfg)
    nt_pre, nt_main = cfg["nt_pre"], cfg["nt_main"]
    f = lambda a: np.asarray(a, dtype=np.float32)
    x, mem = f(x), f(mem)
    n = 8
    shared = {}
    shared["w_f1in"] = _blk(f(ffn1_w_in)[0])
    shared["w_f2in"] = _blk(f(ffn2_w_in)[0])
    k0 = 0
    for i, npart in enumerate(PARTS):
        shared[f"w_f1out{i}"] = _blk(f(ffn1_w_out)[0][k0 * 128:(k0 + npart) * 128])
        shared[f"w_f2out{i}"] = _blk(f(ffn2_w_out)[0][k0 * 128:(k0 + npart) * 128])
        k0 += npart
    shared["w_mixin"] = _blk(f(mix_w_in)[0])
    shared["w_glu"] = _blk(f(ssm_glu_w)[0])
    shared["w_cvout"] = _blk(f(conv_w_out)[0])
    shared["w_mixout"] = _blk(f(mix_w_out)[0])
    shared["w_q"] = _blk(f(xattn_wq)[0])
    shared["w_k"] = _blk(f(xattn_wk)[0])
    shared["w_v"] = _blk(f(xattn_wv)[0])
    shared["w_o"] = _blk(f(xattn_wo)[0])
    shared["ident"] = np.eye(128, dtype=np.float32)
    g = np.zeros((128, 7, KC), np.float32)
    for i, gv in enumerate([ffn1_norm, mix_norm, xattn_norm, mem_norm, ffn2_norm, final_norm]):
        g[:, i, :] = f(gv).reshape(KC, 128).T
    shared["gains"] = g
    are, aim, ldt = f(ssm_a_re)[0], f(ssm_a_im)[0], f(ssm_log_dt)[0]
    bre, bim = f(ssm_b_re)[0], f(ssm_b_im)[0]
    cre, cim = f(ssm_c_re)[0], f(ssm_c_im)[0]
    G, Pn, H = 128, 64, 16
    lane = np.zeros((128, 3, NPAIR), np.float32)
    ar4 = are.reshape(NPAIR, 2, Pn)
    lane[:, 0, :] = ar4.transpose(1, 2, 0).reshape(128, NPAIR)
    lane[:, 1, :] = aim.reshape(NPAIR, 2, Pn).transpose(1, 2, 0).reshape(128, NPAIR)
    lane[:, 2, :] = np.broadcast_to(ldt.reshape(NPAIR, 2, 1), (NPAIR, 2, Pn)).transpose(1, 2, 0).reshape(128, NPAIR)
    shared["ssm_lane"] = lane
    w1 = np.zeros((5, 4, 2, H, 16, 2, Pn), np.float32)
    gidx = (8 * np.arange(16)[None, :, None] + 2 * np.arange(4)[:, None, None] + np.arange(2)[None, None, :])
    A = are[gidx]
    Aim = aim[gidx]
    Ld = np.broadcast_to(ldt[gidx][..., None], A.shape)
    for j, src in enumerate([A, Aim, Ld]):
        w1[j] = np.broadcast_to(src[:, None, None, :, :, :], (4, 2, H, 16, 2, Pn))
    Bre = bre[gidx]
    Bim = bim[gidx]
    for g2 in range(2):
        w1[3][:, g2, :, :, g2, :] = Bre[:, :, g2, :, :].transpose(0, 3, 1, 2)
        w1[4][:, g2, :, :, g2, :] = Bim[:, :, g2, :, :].transpose(0, 3, 1, 2)
    shared["ssm_w1"] = np.ascontiguousarray(w1.reshape(5, 128, 16 * 128).transpose(1, 0, 2))
    w3 = np.zeros((2, 2, Pn, NPAIR, 2, H), np.float32)
    c4 = cre.reshape(NPAIR, 2, H, Pn)
    ci4 = cim.reshape(NPAIR, 2, H, Pn)
    for g2 in range(2):
        w3[0][g2, :, :, g2, :] = c4[:, g2].transpose(2, 0, 1)
        w3[1][g2, :, :, g2, :] = ci4[:, g2].transpose(2, 0, 1)
    shared["ssm_w3"] = np.ascontiguousarray(w3.reshape(2, 128, NPAIR * 32).transpose(1, 0, 2))
    shared["ssm_d"] = np.ascontiguousarray(f(ssm_d)[0].reshape(16, 128).T)
    shared["conv_w"] = np.ascontiguousarray(f(conv_w)[0].reshape(3, 16, 128).transpose(2, 0, 1))

    key = (nt_pre, nt_main)
    if key not in _CACHE:
        _CACHE[key] = build(cfg)
    nc = _CACHE[key]

    half = nt_main * T
    in_maps = []
    for c in range(n):
        b, hf = c // 2, c % 2
        m = dict(shared)
        m["x_main"] = _fm(x[b, hf * 2048: hf * 2048 + max(half, T)])
        m["x_pre"] = _fm(x[b, 0: max(nt_pre, 1) * T])
        m["memT"] = np.ascontiguousarray(mem[b].reshape(NMEM, KC, 128).transpose(1, 2, 0))
        m["maskb"] = np.full((128, 1), float(hf), np.float32)
        in_maps.append(m)
    res = run_bass_kernel_spmd(nc, in_maps, core_ids=list(range(n)))
    out = np.zeros((4, 4096, D), np.float32)
    for c in range(n):
        b, hf = c // 2, c % 2
        o = res.results[c]["out"]
        ntk = o.shape[0]
        tok = o.transpose(0, 3, 1, 2).reshape(ntk * T, D)
        out[b, hf * 2048: hf * 2048 + ntk * T] = tok
    return out
```

```python
import math
from contextlib import ExitStack
import numpy as np
import concourse.bass as bass
import concourse.mybir as mybir
from concourse.bass_utils import run_bass_kernel_spmd

F32 = mybir.dt.float32
BF16 = mybir.dt.bfloat16
I32 = mybir.dt.int32
ALU = mybir.AluOpType
AF = mybir.ActivationFunctionType
AX = mybir.AxisListType

D = 4096
KC = 32
T = 512
DFF = 11008
HC = 86
PARTS = [22, 22, 21, 21]
NMEM = 256
NPAIR = 64
EPS = 1e-6
TWO_PI = 2.0 * math.pi
NWB = 4
NHK = 4
NHN = 3
PF = 3

CFG = {"nt_pre": 4, "nt_main": 4}


class Buf:
    __slots__ = ("name", "lw", "rd", "rd_dma")

    def __init__(self, name):
        self.name = name
        self.lw = None
        self.rd = {}
        self.rd_dma = []


class Op:
    __slots__ = ("eng", "fn", "deps", "dkey", "needs_inc", "val", "idx")


ENGS = ("pe", "act", "dve", "pool", "sp")


class Prog:
    def __init__(self):
        self.ops = {e: [] for e in ENGS}
        self.dcount = {}
        self.nbuf = 0

    def buf(self, name="b"):
        self.nbuf += 1
        return Buf(name)

    def bufs(self, n, name="b"):
        return [self.buf(f"{name}{i}") for i in range(n)]

    def op(self, eng, fn, r=(), w=(), dkey=None):
        o = Op()
        o.eng = eng
        o.fn = fn
        o.dkey = dkey
        o.needs_inc = False
        o.val = None
        o.idx = len(self.ops[eng])
        deps = {}

        def add(d):
            if d is None or d is o:
                return
            if d.dkey is not None:
                deps[("d", id(d))] = d
            else:
                if d.eng == "pe" and eng == "pe":
                    return
                k = ("c", d.eng)
                if k not in deps or deps[k].idx < d.idx:
                    deps[k] = d

        for b in r:
            add(b.lw)
        for b in w:
            add(b.lw)
            for d in b.rd.values():
                add(d)
            for d in b.rd_dma:
                add(d)
        o.deps = list(deps.values())
        for d in o.deps:
            d.needs_inc = True
        if dkey is not None:
            self.dcount[dkey] = self.dcount.get(dkey, 0) + 16
            o.val = self.dcount[dkey]
        for b in r:
            if dkey is not None:
                b.rd_dma.append(o)
            else:
                b.rd[eng] = o
        for b in w:
            b.lw = o
            b.rd = {}
            b.rd_dma = []
        self.ops[eng].append(o)
        return o

    def emit(self, nc, es):
        csem = {e: es.enter_context(nc.semaphore("s_" + e)) for e in ("pe", "act", "dve", "pool")}
        dsem = {k: es.enter_context(nc.semaphore("d_" + k)) for k in self.dcount}
        for e in ("pe", "act", "dve", "pool"):
            c = 0
            for o in self.ops[e]:
                if o.needs_inc:
                    c += 1
                    o.val = c
        block = es.enter_context(nc.Block())
        prog = self

        def run(engname, eng):
            waited = {}
            for o in prog.ops[engname]:
                for d in o.deps:
                    if d.dkey is not None:
                        sem = dsem[d.dkey]
                        v = prog.dcount[d.dkey] if d.dkey == "init" else d.val
                        key = "d_" + d.dkey
                    else:
                        sem = csem[d.eng]
                        v = d.val
                        key = d.eng
                    if waited.get(key, 0) >= v:
                        continue
                    eng.wait_ge(sem, v)
                    waited[key] = v
                ins = o.fn(eng)
                if o.dkey is not None:
                    ins.then_inc(dsem[o.dkey], 16)
                elif o.needs_inc:
                    ins.then_inc(csem[engname], 1)
            if engname == "sp":
                for k, tot in prog.dcount.items():
                    eng.wait_ge(dsem[k], tot)

        @block.tensor
        def _(e):
            run("pe", e)

        @block.scalar
        def _(e):
            run("act", e)

        @block.vector
        def _(e):
            run("dve", e)

        @block.gpsimd
        def _(e):
            run("pool", e)

        @block.sync
        def _(e):
            run("sp", e)


def build(cfg):
    wseq = _build(cfg, None)
    return _build(cfg, wseq)


AHEAD = 3


def _build(cfg, WSEQ):
    dry = WSEQ is None
    REC = []
    nt_pre, nt_main = cfg["nt_pre"], cfg["nt_main"]
    nc = bass.Bass("TRN2", target_bir_lowering=False)
    P = Prog()
    es = ExitStack()

    def din(name, shape):
        return nc.dram_tensor(name, list(shape), F32, kind="ExternalInput").ap()

    x_main = din("x_main", [max(nt_main, 1), KC, 128, T])
    x_pre = din("x_pre", [max(nt_pre, 1), KC, 128, T])
    memT = din("memT", [KC, 128, NMEM])
    maskb = din("maskb", [128, 1])
    ident_d = din("ident", [128, 128])
    gains_d = din("gains", [128, 7, KC])
    WSCR = {}
    WAP = {}

    def dinw(name, shape):
        ap = din(name, shape)
        WAP[name] = ap
        WSCR[name] = nc.dram_tensor("scr_" + name, list(shape), BF16, kind="Internal").ap()
        return (name, ap)

    w_f1in = dinw("w_f1in", [2 * HC, 128, KC * 128])
    w_f1out = [dinw(f"w_f1out{i}", [KC, 128, PARTS[i] * 128]) for i in range(4)]
    w_f2in = dinw("w_f2in", [2 * HC, 128, KC * 128])
    w_f2out = [dinw(f"w_f2out{i}", [KC, 128, PARTS[i] * 128]) for i in range(4)]
    w_mixin = dinw("w_mixin", [128, 128, KC * 128])
    w_glu = dinw("w_glu", [64, 128, 16 * 128])
    w_cvout = dinw("w_cvout", [KC, 128, 16 * 128])
    w_mixout = dinw("w_mixout", [KC, 128, KC * 128])
    w_q = dinw("w_q", [KC, 128, KC * 128])
    w_k = dinw("w_k", [KC, 128, KC * 128])
    w_v = dinw("w_v", [KC, 128, KC * 128])
    w_o = dinw("w_o", [KC, 128, KC * 128])
    ssm_lane = din("ssm_lane", [128, 3, NPAIR])
    ssm_w1 = din("ssm_w1", [128, 5, 16 * 128])
    ssm_w3 = din("ssm_w3", [128, 2, NPAIR * 32])
    ssm_d = din("ssm_d", [128, 16])
    conv_w = din("conv_w", [128, 3, 16])
    out_d = nc.dram_tensor("out", [max(nt_main, 1), KC, 128, T], F32, kind="ExternalOutput").ap()
    hbuf = nc.dram_tensor("hbuf", [KC, 128, T], F32, kind="Internal").ap()
    tab_d = nc.dram_tensor("tab", [NPAIR, 128, 2 * T], F32, kind="Internal").ap()

    def sb(name, shape, dt=F32):
        return es.enter_context(nc.sbuf_tensor(name, list(shape), dt))

    xn = sb("xn", [128, KC, T], BF16)
    ar_a = sb("ar_a", [128, KC, T], BF16)
    ar_b = sb("ar_b", [128, 16, T], BF16)
    ar_c = sb("ar_c", [128, 16, T], BF16)
    tmp = sb("tmp", [128, 6, T], F32)
    srb = sb("srb", [128, 2, T], BF16)
    wst = [sb(f"wst{i}", [128, KC * 64], F32) for i in range(2)]
    wbf = [sb(f"wbf{i}", [128, KC * 128], BF16) for i in range(NWB)]
    hk = [sb(f"hk{i}", [128, T], F32) for i in range(NHK)]
    hn = [sb(f"hn{i}", [128, T], F32) for i in range(NHN)]
    sq = sb("sq", [128, T], F32)
    rstd = sb("rstd", [128, T], F32)
    tb = [sb(f"tb{i}", [128, 2 * T], F32) for i in range(2)]
    w1re = sb("w1re", [128, 16, 128], BF16)
    w1im = sb("w1im", [128, 16, 128], BF16)
    w3re = sb("w3re", [128, NPAIR, 32], BF16)
    w3im = sb("w3im", [128, NPAIR, 32], BF16)
    gains = sb("gains_s", [128, 7, KC])
    ones = sb("ones", [128, 128])
    ident = sb("ident_s", [128, 128])
    epst = sb("epst", [128, 1])
    mb = sb("mb", [128, 1])
    dsk = sb("dsk", [128, 16])
    cw = sb("cw", [128, 3, 16])
    S = sb("S", [128, NPAIR, 2])
    Rl = sb("Rl", [128, NPAIR])
    thl = sb("thl", [128, NPAIR])
    lane_in = sb("lane_in", [128, 3, NPAIR])
    halo = sb("halo", [128, 16, 2])
    cch = sb("cch", [128, T + 2])
    io = sq
    sm = sb("sm", [128, 8])
    psum = [es.enter_context(nc.psum_tensor(f"ps{i}", [128, T], F32)) for i in range(8)]

    B_xn = P.bufs(KC, "xn")
    B_a = P.bufs(KC, "ara")
    B_b = P.bufs(16, "arb")
    B_c = P.bufs(16, "arc")
    B_tmp = P.bufs(6, "tmp")
    B_srb = P.bufs(2, "srb")
    B_wst = P.bufs(2, "wst")
    B_wbf = P.bufs(NWB, "wbf")
    B_hk = P.bufs(NHK, "hk")
    B_hn = P.bufs(NHN, "hn")
    B_sq = P.buf("sq")
    B_rstd = P.buf("rstd")
    B_tbh = P.bufs(4, "tbh")
    B_tb = [[B_tbh[0], B_tbh[1]], [B_tbh[2], B_tbh[3]]]
    B_ps = P.bufs(8, "ps")
    B_const = P.buf("const")
    B_S = P.buf("S")
    B_halo = P.buf("halo")
    B_cch = P.buf("cch")
    B_sm = P.buf("sm")
    B_hbuf = P.bufs(KC, "hbuf")
    B_tab = P.bufs(NPAIR, "tab")
    B_w1 = P.buf("w1")
    B_w3 = P.buf("w3")

    st = {"w": 0, "pc": 0, "preconv": False, "hkw": 0, "issued": 0, "ws": 0, "ps": 0, "hk": 0, "hn": 0, "cast": 0, "tb": 0}

    def next_ps():
        i = st["ps"] % 4
        st["ps"] += 1
        return i

    B_init = P.buf("initscratch")
    B_pc = P.buf("poolconst")
    B_lds = []

    def ld_const(dst_ap, src_ap):
        b = P.buf("ld")
        B_lds.append(b)
        P.op("sp", lambda e, d=dst_ap, s=src_ap: e.dma_start(out=d, in_=s), w=[b], dkey="init")

    ld_const(gains[:], gains_d[:, :, :])
    ld_const(ident[:], ident_d[:, :])
    ld_const(mb[:], maskb[:, :])
    ld_const(dsk[:], ssm_d[:, :])
    ld_const(cw[:], conv_w[:, :, :])
    ld_const(lane_in[:], ssm_lane[:, :, :])
    wa = ar_a[:].rearrange("p a b -> p (a b)").bitcast(F32).rearrange("p (a b) -> p a b", a=4)
    wb_ = xn[:].rearrange("p a b -> p (a b)").bitcast(F32).rearrange("p (a b) -> p a b", a=4)
    wc = ar_b[:].rearrange("p a b -> p (a b)").bitcast(F32).rearrange("p (a b) -> p a b", a=2)
    wd = ar_c[:].rearrange("p a b -> p (a b)").bitcast(F32).rearrange("p (a b) -> p a b", a=2)
    for j in range(5):
        dst = wa[:, j, :] if j < 4 else wb_[:, 0, :]
        ld_const(dst, ssm_w1[:, j, :])
    P.op("pool", lambda e: e.memset(ones[:], 1.0), w=[B_pc])
    P.op("pool", lambda e: e.memset(epst[:], EPS), w=[B_pc])
    P.op("pool", lambda e: e.iota(io[:], pattern=[[1, T]], base=1, channel_multiplier=0,
                                  allow_small_or_imprecise_dtypes=True), w=[B_pc])
    P.op("pool", lambda e: e.memset(S[:], 0.0), w=[B_S])
    P.op("pool", lambda e: e.memset(halo[:], 0.0), w=[B_halo])

    IR = [B_init, B_pc] + B_lds

    def dve(fn, r=(), w=()):
        return P.op("dve", fn, r=list(r), w=list(w))

    def act(fn, r=(), w=()):
        return P.op("act", fn, r=list(r), w=list(w))

    def idve(fn, extra_w=()):
        return P.op("dve", fn, r=IR, w=[B_init] + list(extra_w))

    def iact(fn, extra_w=()):
        return P.op("act", fn, r=IR, w=[B_init] + list(extra_w))

    def range_reduce(x_ap, k_i32, k_f32):
        idve(lambda e: e.tensor_scalar(k_i32, x_ap, 1.0 / TWO_PI, None, ALU.mult))
        idve(lambda e: e.tensor_copy(k_f32, k_i32))
        idve(lambda e: e.scalar_tensor_tensor(x_ap, k_f32, -TWO_PI, x_ap, ALU.mult, ALU.add))
        idve(lambda e: e.tensor_scalar(k_f32, x_ap, math.pi, None, ALU.is_gt))
        idve(lambda e: e.scalar_tensor_tensor(x_ap, k_f32, -TWO_PI, x_ap, ALU.mult, ALU.add))
        idve(lambda e: e.tensor_scalar(k_f32, x_ap, -math.pi, None, ALU.is_lt))
        idve(lambda e: e.scalar_tensor_tensor(x_ap, k_f32, TWO_PI, x_ap, ALU.mult, ALU.add))

    dtl = tmp[:, 0, 0:NPAIR]
    ki_l = tmp[:, 1, 0:NPAIR].bitcast(I32)
    kf_l = tmp[:, 2, 0:NPAIR]
    iact(lambda e: e.activation(dtl, lane_in[:, 2, :], AF.Exp))
    idve(lambda e: e.tensor_tensor(Rl[:], lane_in[:, 0, :], dtl, ALU.mult))
    iact(lambda e: e.activation(Rl[:], Rl[:], AF.Exp))
    idve(lambda e: e.tensor_tensor(thl[:], lane_in[:, 1, :], dtl, ALU.mult))
    range_reduce(thl[:], ki_l, kf_l)

    for pi in range(NPAIR):
        slot = pi % 2
        tbs, Bt = tb[slot], B_tb[slot]
        ang = tmp[:, 0, :]
        ki = tmp[:, 1, :].bitcast(I32)
        kf = tmp[:, 2, :]
        ang2 = tmp[:, 3, :]
        idve(lambda e, pi=pi, ang=ang: e.tensor_scalar(ang, io[:], thl[:, pi:pi + 1], None, ALU.mult))
        idve(lambda e, ang=ang, ang2=ang2: e.tensor_scalar(ang2, ang, math.pi / 2, None, ALU.add))
        range_reduce(ang, ki, kf)
        iact(lambda e, tbs=tbs, ang=ang: e.activation(tbs[:, T:2 * T], ang, AF.Sin), extra_w=Bt)
        range_reduce(ang2, ki, kf)
        iact(lambda e, tbs=tbs, ang2=ang2: e.activation(tbs[:, 0:T], ang2, AF.Sin), extra_w=Bt)
        P.op("sp", lambda e, pi=pi, tbs=tbs: e.dma_start(out=tab_d[pi, :, :], in_=tbs[:]), r=Bt, w=[B_tab[pi]],
             dkey=f"tb{slot}")

    a_re, a_im, ldt, b_re, b_im = wa[:, 0, :], wa[:, 1, :], wa[:, 2, :], wa[:, 3, :], wb_[:, 0, :]
    t1, t2, t3 = wb_[:, 1, :], wb_[:, 2, :], wb_[:, 3, :]
    t4, t5, t6, t7 = wc[:, 0, :], wc[:, 1, :], wd[:, 0, :], wd[:, 1, :]
    iact(lambda e: e.activation(ldt, ldt, AF.Exp))
    idve(lambda e: e.tensor_tensor(t1, a_re, ldt, ALU.mult))
    iact(lambda e: e.activation(t1, t1, AF.Exp))
    idve(lambda e: e.tensor_tensor(t2, a_im, ldt, ALU.mult))
    idve(lambda e: e.tensor_scalar(t3, t2, math.pi / 2, None, ALU.add))
    range_reduce(t2, t6.bitcast(I32), t7)
    range_reduce(t3, t6.bitcast(I32), t7)
    iact(lambda e: e.activation(t2, t2, AF.Sin))
    iact(lambda e: e.activation(t3, t3, AF.Sin))
    idve(lambda e: e.tensor_tensor(t3, t3, t1, ALU.mult))
    idve(lambda e: e.tensor_tensor(t2, t2, t1, ALU.mult))
    idve(lambda e: e.tensor_scalar(t3, t3, -1.0, None, ALU.add))
    idve(lambda e: e.tensor_tensor(t1, a_re, a_re, ALU.mult))
    idve(lambda e: e.tensor_tensor(t4, a_im, a_im, ALU.mult))
    idve(lambda e: e.tensor_tensor(t1, t1, t4, ALU.add))
    idve(lambda e: e.reciprocal(t1, t1))
    idve(lambda e: e.tensor_tensor(t4, t3, a_re, ALU.mult))
    idve(lambda e: e.tensor_tensor(t5, t2, a_im, ALU.mult))
    idve(lambda e: e.tensor_tensor(t4, t4, t5, ALU.add))
    idve(lambda e: e.tensor_tensor(t4, t4, t1, ALU.mult))
    idve(lambda e: e.tensor_tensor(t5, t2, a_re, ALU.mult))
    idve(lambda e: e.tensor_tensor(t6, t3, a_im, ALU.mult))
    idve(lambda e: e.tensor_tensor(t5, t5, t6, ALU.subtract))
    idve(lambda e: e.tensor_tensor(t5, t5, t1, ALU.mult))
    idve(lambda e: e.tensor_tensor(t6, t4, b_re, ALU.mult))
    idve(lambda e: e.tensor_tensor(t7, t5, b_im, ALU.mult))
    idve(lambda e: e.tensor_tensor(w1re[:].rearrange("p a b -> p (a b)"), t6, t7, ALU.subtract), extra_w=[B_w1])
    idve(lambda e: e.tensor_tensor(t6, t4, b_im, ALU.mult))
    idve(lambda e: e.tensor_tensor(t7, t5, b_re, ALU.mult))
    idve(lambda e: e.tensor_tensor(w1im[:].rearrange("p a b -> p (a b)"), t6, t7, ALU.add), extra_w=[B_w1])
    P.op("sp", lambda e: e.dma_start(out=t6, in_=ssm_w3[:, 0, :]), r=[B_init], w=[B_init], dkey="i2a")
    P.op("sp", lambda e: e.dma_start(out=t7, in_=ssm_w3[:, 1, :]), r=[B_init], w=[B_init], dkey="i2b")
    idve(lambda e: e.tensor_copy(w3re[:].rearrange("p a b -> p (a b)"), t6), extra_w=[B_w3])
    idve(lambda e: e.tensor_scalar(w3im[:].rearrange("p a b -> p (a b)"), t7, -1.0, None, ALU.mult), extra_w=[B_w3])
    idve(lambda e: e.memset(sm[:, 7:8], 0.0), extra_w=[B_const, B_sq] + B_a + B_xn + B_b + B_c + B_tmp)

    SEEN = {}
    PEND = []

    def flush_store():
        i, n, scr, blk, b = PEND.pop(0)
        P.op("sp", lambda e, i=i, n=n, scr=scr, blk=blk: e.dma_start(out=scr[blk, :, :], in_=wbf[i][:, 0:n]),
             r=[B_wbf[i]], w=[b], dkey=f"wb{i}")

    def issue_load(k):
        wref, blk, ncols = WSEQ[k]
        wname = wref[0]
        wap = WAP[wname]
        key = (wname, blk)
        i = k % NWB
        n = ncols
        while len(PEND) > 2 or any(p[0] == i for p in PEND):
            flush_store()
        if key not in SEEN:
            nh = n // 2
            for j in range(2):
                P.op("sp", lambda e, j=j, nh=nh, wap=wap, blk=blk: e.dma_start(out=wst[j][:, 0:nh],
                                                                              in_=wap[blk, :, j * nh:(j + 1) * nh]),
                     w=[B_wst[j]], dkey=f"ws{j}")
                ce = "dve" if (st["cast"] % 3 == 2) else "act"
                st["cast"] += 1
                if ce == "act":
                    P.op("act", lambda e, i=i, j=j, nh=nh: e.activation(wbf[i][:, j * nh:(j + 1) * nh], wst[j][:, 0:nh], AF.Copy),
                         r=[B_wst[j]], w=[B_wbf[i]])
                else:
                    P.op("dve", lambda e, i=i, j=j, nh=nh: e.tensor_copy(wbf[i][:, j * nh:(j + 1) * nh], wst[j][:, 0:nh]),
                         r=[B_wst[j]], w=[B_wbf[i]])
            b = P.buf("scr")
            SEEN[key] = b
            PEND.append((i, n, WSCR[wname], blk, b))
        else:
            for pe_ in list(PEND):
                if pe_[4] is SEEN[key]:
                    while PEND:
                        flush_store()
            scr = WSCR[wname]
            P.op("sp", lambda e, i=i, n=n, scr=scr, blk=blk: e.dma_start(out=wbf[i][:, 0:n], in_=scr[blk, :, :]),
                 r=[SEEN[key]], w=[B_wbf[i]], dkey=f"wl{i}")

    def wload(wref, blk, ncols):
        k = st["w"]
        st["w"] += 1
        if dry:
            REC.append((wref, blk, ncols))
            return wbf[k % NWB], B_wbf[k % NWB]
        assert WSEQ[k][1] == blk and WSEQ[k][2] == ncols and WSEQ[k][0][0] == wref[0]
        while st["issued"] < min(k + AHEAD + 1, len(WSEQ)):
            issue_load(st["issued"])
            st["issued"] += 1
        return wbf[k % NWB], B_wbf[k % NWB]

    pst = [ar_b[:, 0:8, :].rearrange("p a b -> p (a b)").bitcast(F32), ar_b[:, 8:16, :].rearrange("p a b -> p (a b)").bitcast(F32)]
    Bpst = [B_b[0:8], B_b[8:16]]
    pcb = [ar_c[:, 0:8, :].rearrange("p a b -> p (a b)"), ar_c[:, 8:16, :].rearrange("p a b -> p (a b)")]
    Bpcb = [B_c[0:8], B_c[8:16]]
    PREQ = []
    PEND2 = []

    def flush_store2():
        slot, n, scr, blk, b = PEND2.pop(0)
        P.op("sp", lambda e, slot=slot, n=n, scr=scr, blk=blk: e.dma_start(out=scr[blk, :, :], in_=pcb[slot][:, 0:n]),
             r=Bpcb[slot], w=[b], dkey=f"pc{slot}")

    def preconvert_one():
        if dry or not st["preconv"]:
            return
        while PREQ:
            wname, blk, n = PREQ.pop(0)
            if (wname, blk) not in SEEN:
                break
        else:
            return
        slot = st["pc"] % 2
        st["pc"] += 1
        while any(p[0] == slot for p in PEND2):
            flush_store2()
        nh = n // 2
        wap = WAP[wname]
        for j in range(2):
            P.op("sp", lambda e, j=j, nh=nh, wap=wap, blk=blk: e.dma_start(out=pst[j][:, 0:nh],
                                                                          in_=wap[blk, :, j * nh:(j + 1) * nh]),
                 w=Bpst[j], dkey=f"ps{j}")
            P.op("act", lambda e, j=j, nh=nh, slot=slot: e.activation(pcb[slot][:, j * nh:(j + 1) * nh], pst[j][:, 0:nh], AF.Copy),
                 r=Bpst[j], w=Bpcb[slot])
        b = P.buf("scr")
        SEEN[(wname, blk)] = b
        PEND2.append((slot, n, WSCR[wname], blk, b))

    def mm_group(ps_i, wt, Bw, rhs_list, n=T, ps_cols=None):
        nk = len(rhs_list)
        pcols = ps_cols if ps_cols is not None else slice(0, n)
        for k, (rap, rb) in enumerate(rhs_list):
            P.op("pe", lambda e, k=k, rap=rap, ps_i=ps_i, wt=wt, nk=nk, pcols=pcols: e.matmul(
                psum[ps_i][:, pcols], wt[:, k * 128:(k + 1) * 128], rap, start=(k == 0), stop=(k == nk - 1)),
                r=[Bw, rb], w=[B_ps[ps_i]])

    tbh = [tb[0][:, 0:T], tb[0][:, T:2 * T], tb[1][:, 0:T], tb[1][:, T:2 * T]]

    def hk_load(src_ap, n=T, extra_r=(), wide=False):
        if wide:
            i = st["hkw"] % (NHK + 4)
            st["hkw"] += 1
        else:
            i = st["hk"] % NHK
            st["hk"] += 1
        if i < NHK:
            buf, Bb, key = hk[i], B_hk[i], f"hk{i}"
        else:
            buf, Bb, key = tbh[i - NHK], B_tbh[i - NHK], f"th{i - NHK}"
        P.op("sp", lambda e, buf=buf, s=src_ap, n=n: e.dma_start(out=buf[:, 0:n], in_=s), r=list(extra_r),
             w=[Bb], dkey=key)
        return buf, Bb

    def rms_stats(chunk_src, n, nchunks=KC):
        ps_i = next_ps()
        for k in range(nchunks):
            src, rb = chunk_src(k)
            hkt, Bh = hk_load(src, n, rb, wide=True)
            act(lambda e, hkt=hkt, n=n: e.activation(sq[:, 0:n], hkt[:, 0:n], AF.Square), r=[Bh], w=[B_sq])
            P.op("pe", lambda e, k=k, n=n, ps_i=ps_i, nchunks=nchunks: e.matmul(
                psum[ps_i][:, 0:n], ones[:, :], sq[:, 0:n], start=(k == 0), stop=(k == nchunks - 1)),
                r=[B_sq, B_const], w=[B_ps[ps_i]])
        act(lambda e, n=n, ps_i=ps_i: e.activation(rstd[:, 0:n], psum[ps_i][:, 0:n], AF.Sqrt, bias=epst[:, 0:1],
                                                     scale=1.0 / D), r=[B_ps[ps_i], B_const], w=[B_rstd])
        dve(lambda e, n=n: e.reciprocal(rstd[:, 0:n], rstd[:, 0:n]), r=[B_rstd], w=[B_rstd])

    def norm_from_hbuf(gi):
        rms_stats(lambda k: (hbuf[k, :, :], [B_hbuf[k]]), T)
        for k in range(KC):
            hkt, Bh = hk_load(hbuf[k, :, :], T, [B_hbuf[k]], wide=True)
            dve(lambda e, k=k, hkt=hkt: e.scalar_tensor_tensor(xn[:, k, :], hkt[:, :], gains[:, gi, k:k + 1], rstd[:, :],
                                                               ALU.mult, ALU.mult),
                r=[Bh, B_rstd, B_const], w=[B_xn[k]])

    HKQ = {}

    def ep_prefetch(m):
        if m < KC:
            HKQ[m] = hk_load(hbuf[m, :, :], T, [B_hbuf[m]])

    def ep_begin():
        for m in range(PF):
            ep_prefetch(m)

    def epilogue(ps_i, m, scale):
        ep_prefetch(m + PF)
        hkt, Bh = HKQ.pop(m)
        j = st["hn"] % NHN
        st["hn"] += 1
        dve(lambda e, hkt=hkt, j=j, ps_i=ps_i, scale=scale: e.scalar_tensor_tensor(
            hn[j][:, :], psum[ps_i][:, :], scale, hkt[:, :], ALU.mult, ALU.add),
            r=[B_ps[ps_i], Bh], w=[B_hn[j]])
        P.op("sp", lambda e, j=j, m=m: e.dma_start(out=hbuf[m, :, :], in_=hn[j][:, :]), r=[B_hn[j]], w=[B_hbuf[m]],
             dkey=f"hn{j}")

    xn_rhs = [(xn[:, k, :], B_xn[k]) for k in range(KC)]

    def ffn(w_in, w_out, gi):
        norm_from_hbuf(gi)
        hc0 = 0
        for part, npart in enumerate(PARTS):
            for jj in range(npart):
                j = hc0 + jj
                wa_t, Bwa = wload(w_in, j, KC * 128)
                pa = next_ps()
                mm_group(pa, wa_t, Bwa, xn_rhs)
                wb_t, Bwb = wload(w_in, HC + j, KC * 128)
                pb = next_ps()
                mm_group(pb, wb_t, Bwb, xn_rhs)
                ti = jj % 2
                act(lambda e, pa=pa, ti=ti: e.activation(tmp[:, ti, :], psum[pa][:, :], AF.Silu),
                    r=[B_ps[pa]], w=[B_tmp[ti]])
                dve(lambda e, pb=pb, ti=ti, jj=jj: e.tensor_tensor(ar_a[:, jj, :], tmp[:, ti, :], psum[pb][:, :], ALU.mult),
                    r=[B_ps[pb], B_tmp[ti]], w=[B_a[jj]])
                preconvert_one()
            g_rhs = [(ar_a[:, jj, :], B_a[jj]) for jj in range(npart)]
            ep_begin()
            for m in range(KC):
                wo_t, Bwo = wload(w_out[part], m, npart * 128)
                po = next_ps()
                mm_group(po, wo_t, Bwo, g_rhs)
                epilogue(po, m, 0.5)
                preconvert_one()
            hc0 += npart

    def ssm(full, B_u):
        for c in range(16):
            yps = next_ps() if full else None
            for q in range(4):
                pi = 4 * c + q
                par = pi % 2
                pA, pB = 4 + 2 * par, 5 + 2 * par
                rows = slice(32 * q, 32 * q + 32)
                P.op("pe", lambda e, c=c, q=q, rows=rows, pA=pA: e.matmul(
                    psum[pA][:, :], w1re[rows, c, :], ar_a[rows, c, :], start=True, stop=True, tile_position=(32 * q, 0)),
                    r=[B_w1, B_u[c]], w=[B_ps[pA]])
                P.op("pe", lambda e, c=c, q=q, rows=rows, pB=pB: e.matmul(
                    psum[pB][:, :], w1im[rows, c, :], ar_a[rows, c, :], start=True, stop=True, tile_position=(32 * q, 0)),
                    r=[B_w1, B_u[c]], w=[B_ps[pB]])
                ts = st["tb"] % 2
                st["tb"] += 1
                P.op("sp", lambda e, ts=ts, pi=pi: e.dma_start(out=tb[ts][:], in_=tab_d[pi, :, :]), r=[B_tab[pi]],
                     w=B_tb[ts], dkey=f"tb{ts}")
                cs, sn = tb[ts][:, 0:T], tb[ts][:, T:2 * T]
                Bt = B_tb[ts]
                A_, B_ = psum[pA][:, :], psum[pB][:, :]
                t = [tmp[:, i, :] for i in range(6)]
                TT = lambda o, a, b, op, r, w: dve(lambda e, o=o, a=a, b=b, op=op: e.tensor_tensor(o, a, b, op), r=r, w=w)
                TT(t[0], A_, cs, ALU.mult, [B_ps[pA]] + Bt, [B_tmp[0]])
                TT(t[1], B_, sn, ALU.mult, [B_ps[pB]] + Bt, [B_tmp[1]])
                TT(t[0], t[0], t[1], ALU.add, [B_tmp[0], B_tmp[1]], [B_tmp[0]])
                TT(t[2], B_, cs, ALU.mult, [B_ps[pB]] + Bt, [B_tmp[2]])
                TT(t[3], A_, sn, ALU.mult, [B_ps[pA]] + Bt, [B_tmp[3]])
                TT(t[2], t[2], t[3], ALU.subtract, [B_tmp[2], B_tmp[3]], [B_tmp[2]])
                rbc = Rl[:, pi:pi + 1].to_broadcast([128, T])
                dve(lambda e, pi=pi, rbc=rbc, o=t[1], i_=t[0]: e.tensor_tensor_scan(o, rbc, i_, S[:, pi, 0:1], ALU.mult, ALU.add),
                    r=[B_tmp[0], B_S, B_init], w=[B_tmp[1]])
                dve(lambda e, pi=pi, rbc=rbc, o=t[3], i_=t[2]: e.tensor_tensor_scan(o, rbc, i_, S[:, pi, 1:2], ALU.mult, ALU.add),
                    r=[B_tmp[2], B_S, B_init], w=[B_tmp[3]])
                n0 = 0 if full else T - 1
                sl = slice(n0, T)
                TT(t[0][:, sl], tb[ts][:, n0:T], t[1][:, sl], ALU.mult, [B_tmp[1]] + Bt, [B_tmp[0]])
                TT(t[2][:, sl], tb[ts][:, T + n0:2 * T], t[3][:, sl], ALU.mult, [B_tmp[3]] + Bt, [B_tmp[2]])
                TT(t[4][:, sl], t[0][:, sl], t[2][:, sl], ALU.subtract, [B_tmp[0], B_tmp[2]], [B_tmp[4]])
                TT(t[0][:, sl], tb[ts][:, T + n0:2 * T], t[1][:, sl], ALU.mult, [B_tmp[1]] + Bt, [B_tmp[0]])
                TT(t[2][:, sl], tb[ts][:, n0:T], t[3][:, sl], ALU.mult, [B_tmp[3]] + Bt, [B_tmp[2]])
                TT(t[5][:, sl], t[0][:, sl], t[2][:, sl], ALU.add, [B_tmp[0], B_tmp[2]], [B_tmp[5]])
                dve(lambda e, pi=pi: e.tensor_copy(S[:, pi, 0:1], tmp[:, 4, T - 1:T]), r=[B_tmp[4]], w=[B_S])
                dve(lambda e, pi=pi: e.tensor_copy(S[:, pi, 1:2], tmp[:, 5, T - 1:T]), r=[B_tmp[5]], w=[B_S])
                if full:
                    act(lambda e: e.activation(srb[:, 0, :], tmp[:, 4, :], AF.Copy), r=[B_tmp[4]], w=[B_srb[0]])
                    act(lambda e: e.activation(srb[:, 1, :], tmp[:, 5, :], AF.Copy), r=[B_tmp[5]], w=[B_srb[1]])
                    P.op("pe", lambda e, pi=pi, rows=rows, yps=yps, q=q: e.matmul(
                        psum[yps][rows, :], w3re[:, pi, :], srb[:, 0, :], start=True, stop=False, tile_position=(0, 32 * q)),
                        r=[B_w3, B_srb[0]], w=[B_ps[yps]])
                    P.op("pe", lambda e, pi=pi, rows=rows, yps=yps, q=q: e.matmul(
                        psum[yps][rows, :], w3im[:, pi, :], srb[:, 1, :], start=False, stop=True, tile_position=(0, 32 * q)),
                        r=[B_w3, B_srb[1]], w=[B_ps[yps]])
            if full:
                dve(lambda e, c=c, yps=yps: e.scalar_tensor_tensor(tmp[:, 0, :], ar_a[:, c, :], dsk[:, c:c + 1], psum[yps][:, :],
                                                                   ALU.mult, ALU.add),
                    r=[B_u[c], B_ps[yps], B_const], w=[B_tmp[0]])
                act(lambda e, c=c: e.activation(ar_c[:, c, :], tmp[:, 0, :], AF.Gelu), r=[B_tmp[0]], w=[B_c[c]])

    def conv_branch(full):
        for c in range(16):
            wt, Bw = wload(w_mixin, 32 + c, KC * 128)
            pcc = next_ps()
            mm_group(pcc, wt, Bw, xn_rhs)
            act(lambda e, pcc=pcc: e.activation(tmp[:, 0, :], psum[pcc][:, :], AF.Copy), r=[B_ps[pcc]], w=[B_tmp[0]])
            wt, Bw = wload(w_mixin, 48 + c, KC * 128)
            pch = next_ps()
            mm_group(pch, wt, Bw, xn_rhs)
            dve(lambda e, c=c: e.tensor_copy(cch[:, 0:2], halo[:, c, :]), r=[B_halo], w=[B_cch])
            dve(lambda e, pch=pch: e.tensor_tensor(cch[:, 2:T + 2], tmp[:, 0, :], psum[pch][:, :], ALU.mult),
                r=[B_tmp[0], B_ps[pch]], w=[B_cch])
            dve(lambda e, c=c: e.tensor_copy(halo[:, c, :], cch[:, T:T + 2]), r=[B_cch], w=[B_halo])
            if not full:
                continue
            wt, Bw = wload(w_mixin, 16 + c, KC * 128)
            pcb = next_ps()
            mm_group(pcb, wt, Bw, xn_rhs)
            dve(lambda e, c=c: e.tensor_scalar(tmp[:, 1, :], cch[:, 0:T], cw[:, 0, c:c + 1], None, ALU.mult),
                r=[B_cch, B_const], w=[B_tmp[1]])
            dve(lambda e, c=c: e.scalar_tensor_tensor(tmp[:, 1, :], cch[:, 1:T + 1], cw[:, 1, c:c + 1], tmp[:, 1, :],
                                                      ALU.mult, ALU.add), r=[B_cch, B_const, B_tmp[1]], w=[B_tmp[1]])
            dve(lambda e, c=c: e.scalar_tensor_tensor(tmp[:, 1, :], cch[:, 2:T + 2], cw[:, 2, c:c + 1], tmp[:, 1, :],
                                                      ALU.mult, ALU.add), r=[B_cch, B_const, B_tmp[1]], w=[B_tmp[1]])
            dve(lambda e, c=c, pcb=pcb: e.tensor_tensor(ar_b[:, c, :], tmp[:, 1, :], psum[pcb][:, :], ALU.mult),
                r=[B_tmp[1], B_ps[pcb]], w=[B_b[c]])

    def u_proj():
        for c in range(16):
            wt, Bw = wload(w_mixin, c, KC * 128)
            pu = next_ps()
            mm_group(pu, wt, Bw, xn_rhs)
            act(lambda e, c=c, pu=pu: e.activation(ar_a[:, c, :], psum[pu][:, :], AF.Copy), r=[B_ps[pu]], w=[B_a[c]])

    def mixer():
        norm_from_hbuf(1)
        u_proj()
        conv_branch(True)
        ssm(True, B_a)
        ys_rhs = [(ar_c[:, c, :], B_c[c]) for c in range(16)]
        cv_rhs = [(ar_b[:, c, :], B_b[c]) for c in range(16)]
        for m in range(KC):
            wt, Bw = wload(w_mixin, 64 + m, KC * 128)
            p1 = next_ps()
            mm_group(p1, wt, Bw, xn_rhs)
            act(lambda e, p1=p1: e.activation(tmp[:, 0, :], psum[p1][:, :], AF.Sigmoid), r=[B_ps[p1]], w=[B_tmp[0]])
            wt, Bw = wload(w_glu, m, 16 * 128)
            p2 = next_ps()
            mm_group(p2, wt, Bw, ys_rhs)
            dve(lambda e, p2=p2: e.tensor_tensor(tmp[:, 1, :], tmp[:, 0, :], psum[p2][:, :], ALU.mult),
                r=[B_tmp[0], B_ps[p2]], w=[B_tmp[1]])
            wt, Bw = wload(w_glu, 32 + m, 16 * 128)
            p3 = next_ps()
            mm_group(p3, wt, Bw, ys_rhs)
            act(lambda e, p3=p3: e.activation(tmp[:, 2, :], psum[p3][:, :], AF.Sigmoid), r=[B_ps[p3]], w=[B_tmp[2]])
            dve(lambda e: e.tensor_tensor(tmp[:, 1, :], tmp[:, 1, :], tmp[:, 2, :], ALU.mult),
                r=[B_tmp[1], B_tmp[2]], w=[B_tmp[1]])
            wt, Bw = wload(w_mixin, 96 + m, KC * 128)
            p4 = next_ps()
            mm_group(p4, wt, Bw, xn_rhs)
            act(lambda e, p4=p4: e.activation(tmp[:, 3, :], psum[p4][:, :], AF.Sigmoid), r=[B_ps[p4]], w=[B_tmp[3]])
            wt, Bw = wload(w_cvout, m, 16 * 128)
            p5 = next_ps()
            mm_group(p5, wt, Bw, cv_rhs)
            dve(lambda e, p5=p5: e.tensor_tensor(tmp[:, 4, :], tmp[:, 3, :], psum[p5][:, :], ALU.mult),
                r=[B_tmp[3], B_ps[p5]], w=[B_tmp[4]])
            dve(lambda e, m=m: e.tensor_tensor(ar_a[:, m, :], tmp[:, 1, :], tmp[:, 4, :], ALU.add),
                r=[B_tmp[1], B_tmp[4]], w=[B_a[m]])
        mg_rhs = [(ar_a[:, k, :], B_a[k]) for k in range(KC)]
        ep_begin()
        for m in range(KC):
            wt, Bw = wload(w_mixout, m, KC * 128)
            po = next_ps()
            mm_group(po, wt, Bw, mg_rhs)
            epilogue(po, m, 1.0)

    def xattn():
        norm_from_hbuf(2)
        memn = ar_b[:].rearrange("p a b -> p (a b)").rearrange("p (a b) -> p a b", a=KC)
        rms_stats(lambda k: (memT[k, :, :], []), NMEM)
        for k in range(KC):
            hkt, Bh = hk_load(memT[k, :, :], NMEM, (), wide=True)
            dve(lambda e, k=k, hkt=hkt: e.scalar_tensor_tensor(memn[:, k, :], hkt[:, 0:NMEM], gains[:, 3, k:k + 1],
                                                               rstd[:, 0:NMEM], ALU.mult, ALU.mult),
                r=[Bh, B_rstd, B_const], w=[B_b[k // 2]])
        mem_rhs = [(memn[:, k, :], B_b[k // 2]) for k in range(KC)]
        arc_flat = ar_c[:].rearrange("p a b -> p (a b)")
        KTh = arc_flat[:, 0:2048].rearrange("p (a b) -> p a b", a=8)
        Vh = arc_flat[:, 2048:4096].rearrange("p (a b) -> p a b", a=2)
        qTh = arc_flat[:, 4096:8192].rearrange("p (a b) -> p a b", a=8)
        B_KT, B_V, B_q = B_c[0], B_c[1], B_c[2]
        for h in range(4):
            for dc in range(8):
                wt, Bw = wload(w_k, h * 8 + dc, KC * 128)
                pk = next_ps()
                mm_group(pk, wt, Bw, mem_rhs, n=NMEM)
                act(lambda e, dc=dc, pk=pk: e.activation(KTh[:, dc, :], psum[pk][:, 0:NMEM], AF.Copy), r=[B_ps[pk]], w=[B_KT])
            for dc in range(8):
                wt, Bw = wload(w_v, h * 8 + dc, KC * 128)
                pv = next_ps()
                for mc in range(2):
                    for k in range(KC):
                        P.op("pe", lambda e, k=k, mc=mc, pv=pv, wt=wt: e.matmul(
                            psum[pv][:, mc * 128:(mc + 1) * 128], memn[:, k, mc * 128:(mc + 1) * 128],
                            wt[:, k * 128:(k + 1) * 128], start=(k == 0), stop=(k == KC - 1)),
                            r=[Bw, B_b[k // 2]], w=[B_ps[pv]])
                for mc in range(2):
                    act(lambda e, dc=dc, mc=mc, pv=pv: e.activation(Vh[:, mc, dc * 128:(dc + 1) * 128],
                                                                     psum[pv][:, mc * 128:(mc + 1) * 128], AF.Copy),
                        r=[B_ps[pv]], w=[B_V])
            for dc in range(8):
                wt, Bw = wload(w_q, h * 8 + dc, KC * 128)
                pq = next_ps()
                mm_group(pq, wt, Bw, xn_rhs)
                act(lambda e, dc=dc, pq=pq: e.activation(qTh[:, dc, :], psum[pq][:, :], AF.Copy, scale=1.0 / 32.0),
                    r=[B_ps[pq]], w=[B_q])
            for tc in range(4):
                pss = next_ps()
                for dc in range(8):
                    P.op("pe", lambda e, dc=dc, tc=tc, pss=pss: e.matmul(
                        psum[pss][:, 0:NMEM], qTh[:, dc, tc * 128:(tc + 1) * 128], KTh[:, dc, :],
                        start=(dc == 0), stop=(dc == 7)), r=[B_q, B_KT], w=[B_ps[pss]])
                dve(lambda e, pss=pss: e.reduce_max(sm[:, 0:1], psum[pss][:, 0:NMEM], AX.X), r=[B_ps[pss]], w=[B_sm])
                dve(lambda e: e.tensor_scalar(sm[:, 1:2], sm[:, 0:1], -1.0, None, ALU.mult), r=[B_sm], w=[B_sm])
                act(lambda e, pss=pss: e.activation(tmp[:, 0, 0:NMEM], psum[pss][:, 0:NMEM], AF.Exp, bias=sm[:, 1:2],
                                                     accum_out=sm[:, 2:3]), r=[B_ps[pss], B_sm], w=[B_tmp[0], B_sm])
                dve(lambda e: e.reciprocal(sm[:, 3:4], sm[:, 2:3]), r=[B_sm], w=[B_sm])
                dve(lambda e: e.tensor_scalar(tmp[:, 1, 0:NMEM], tmp[:, 0, 0:NMEM], sm[:, 3:4], None, ALU.mult),
                    r=[B_tmp[0], B_sm], w=[B_tmp[1]])
                for mc in range(2):
                    ppt = next_ps()
                    P.op("pe", lambda e, mc=mc, ppt=ppt: e.matmul(psum[ppt][:, 0:128], tmp[:, 1, mc * 128:(mc + 1) * 128],
                                                                  ident[:, :], start=True, stop=True),
                         r=[B_tmp[1], B_const], w=[B_ps[ppt]])
                    act(lambda e, mc=mc, tc=tc, ppt=ppt: e.activation(srb[:, mc, tc * 128:(tc + 1) * 128], psum[ppt][:, 0:128],
                                                                       AF.Copy), r=[B_ps[ppt]], w=[B_srb[mc]])
            for dc in range(8):
                po_ = next_ps()
                for mc in range(2):
                    P.op("pe", lambda e, dc=dc, mc=mc, po_=po_: e.matmul(
                        psum[po_][:, :], Vh[:, mc, dc * 128:(dc + 1) * 128], srb[:, mc, :], start=(mc == 0), stop=(mc == 1)),
                        r=[B_V, B_srb[mc]], w=[B_ps[po_]])
                act(lambda e, dc=dc, h=h, po_=po_: e.activation(ar_a[:, h * 8 + dc, :], psum[po_][:, :], AF.Copy),
                    r=[B_ps[po_]], w=[B_a[h * 8 + dc]])
        o_rhs = [(ar_a[:, k, :], B_a[k]) for k in range(KC)]
        ep_begin()
        for m in range(KC):
            wt, Bw = wload(w_o, m, KC * 128)
            po = next_ps()
            mm_group(po, wt, Bw, o_rhs)
            epilogue(po, m, 1.0)

    def final_norm(ti):
        rms_stats(lambda k: (hbuf[k, :, :], [B_hbuf[k]]), T)
        for k in range(KC):
            hkt, Bh = hk_load(hbuf[k, :, :], T, [B_hbuf[k]], wide=True)
            j = st["hn"] % NHN
            st["hn"] += 1
            dve(lambda e, k=k, hkt=hkt, j=j: e.scalar_tensor_tensor(hn[j][:, :], hkt[:, :], gains[:, 5, k:k + 1], rstd[:, :],
                                                                    ALU.mult, ALU.mult),
                r=[Bh, B_rstd, B_const], w=[B_hn[j]])
            P.op("sp", lambda e, j=j, k=k, ti=ti: e.dma_start(out=out_d[ti, k, :, :], in_=hn[j][:, :]), r=[B_hn[j]],
                 dkey=f"hn{j}")

    def load_h(src):
        P.op("sp", lambda e, src=src: e.dma_start(out=hbuf[:, :, :], in_=src), w=B_hbuf, dkey="hcopy")

    hc0_ = 0
    for part_, npart_ in enumerate(PARTS):
        for jj_ in range(npart_):
            PREQ.append((w_f2in[0], hc0_ + jj_, KC * 128))
            PREQ.append((w_f2in[0], HC + hc0_ + jj_, KC * 128))
        for m_ in range(KC):
            PREQ.append((w_f2out[part_][0], m_, npart_ * 128))
        hc0_ += npart_
    for ti in range(nt_pre):
        st["preconv"] = ti >= 1
        load_h(x_pre[ti, :, :, :])
        ffn(w_f1in, w_f1out, 0)
        norm_from_hbuf(1)
        u_proj()
        if ti == nt_pre - 1:
            conv_branch(False)
        ssm(False, B_a)
    st["preconv"] = False
    while PEND2:
        flush_store2()
    if nt_pre > 0:
        dve(lambda e: e.tensor_scalar(S[:].rearrange("p a b -> p (a b)"), S[:].rearrange("p a b -> p (a b)"), mb[:, 0:1],
                                      None, ALU.mult), r=[B_S, B_const], w=[B_S])
        dve(lambda e: e.tensor_scalar(halo[:].rearrange("p a b -> p (a b)"), halo[:].rearrange("p a b -> p (a b)"),
                                      mb[:, 0:1], None, ALU.mult), r=[B_halo, B_const], w=[B_halo])
    for ti in range(nt_main):
        load_h(x_main[ti, :, :, :])
        ffn(w_f1in, w_f1out, 0)
        mixer()
        xattn()
        ffn(w_f2in, w_f2out, 4)
        final_norm(ti)

    while PEND:
        flush_store()
    if dry:
        es.close()
        return REC
    P.emit(nc, es)
    es.close()
    return nc


def _blk(Wm, kparts=None):
    K, N = Wm.shape
    kc, nm = K // 128, N // 128
    return np.ascontiguousarray(Wm.reshape(kc, 128, nm, 128).transpose(2, 1, 0, 3).reshape(nm, 128, kc * 128))


def _fm(xtok):
    nt = xtok.shape[0] // T
    return np.ascontiguousarray(xtok.reshape(nt, T, KC, 128).transpose(0, 2, 3, 1))


_CACHE = {}


def kernel(x, mem, ffn1_norm, ffn1_w_in, ffn1_w_out, mix_norm, mix_w_in,
           ssm_a_re, ssm_a_im, ssm_log_dt, ssm_b_re, ssm_b_im, ssm_c_re, ssm_c_im,
           ssm_d, ssm_glu_w, conv_w, conv_w_out, mix_w_out,
           xattn_norm, mem_norm, xattn_wq, xattn_wk, xattn_wv, xattn_wo,
           ffn2_norm, ffn2_w_in, ffn2_w_out, final_norm, _cfg=None):
    cfg = dict(CFG if _cfg is None else _cfg)
    nt_pre, nt_main = cfg["nt_pre"], cfg["nt_main"]
    f = lambda a: np.asarray(a, dtype=np.float32)
    x, mem = f(x), f(mem)
    n = 8
    shared = {}
    shared["w_f1in"] = _blk(f(ffn1_w_in)[0])
    shared["w_f2in"] = _blk(f(ffn2_w_in)[0])
    k0 = 0
    for i, npart in enumerate(PARTS):
        shared[f"w_f1out{i}"] = _blk(f(ffn1_w_out)[0][k0 * 128:(k0 + npart) * 128])
        shared[f"w_f2out{i}"] = _blk(f(ffn2_w_out)[0][k0 * 128:(k0 + npart) * 128])
        k0 += npart
    shared["w_mixin"] = _blk(f(mix_w_in)[0])
    shared["w_glu"] = _blk(f(ssm_glu_w)[0])
    shared["w_cvout"] = _blk(f(conv_w_out)[0])
    shared["w_mixout"] = _blk(f(mix_w_out)[0])
    shared["w_q"] = _blk(f(xattn_wq)[0])
    shared["w_k"] = _blk(f(xattn_wk)[0])
    shared["w_v"] = _blk(f(xattn_wv)[0])
    shared["w_o"] = _blk(f(xattn_wo)[0])
    shared["ident"] = np.eye(128, dtype=np.float32)
    g = np.zeros((128, 7, KC), np.float32)
    for i, gv in enumerate([ffn1_norm, mix_norm, xattn_norm, mem_norm, ffn2_norm, final_norm]):
        g[:, i, :] = f(gv).reshape(KC, 128).T
    shared["gains"] = g
    are, aim, ldt = f(ssm_a_re)[0], f(ssm_a_im)[0], f(ssm_log_dt)[0]
    bre, bim = f(ssm_b_re)[0], f(ssm_b_im)[0]
    cre, cim = f(ssm_c_re)[0], f(ssm_c_im)[0]
    G, Pn, H = 128, 64, 16
    lane = np.zeros((128, 3, NPAIR), np.float32)
    ar4 = are.reshape(NPAIR, 2, Pn)
    lane[:, 0, :] = ar4.transpose(1, 2, 0).reshape(128, NPAIR)
    lane[:, 1, :] = aim.reshape(NPAIR, 2, Pn).transpose(1, 2, 0).reshape(128, NPAIR)
    lane[:, 2, :] = np.broadcast_to(ldt.reshape(NPAIR, 2, 1), (NPAIR, 2, Pn)).transpose(1, 2, 0).reshape(128, NPAIR)
    shared["ssm_lane"] = lane
    w1 = np.zeros((5, 4, 2, H, 16, 2, Pn), np.float32)
    gidx = (8 * np.arange(16)[None, :, None] + 2 * np.arange(4)[:, None, None] + np.arange(2)[None, None, :])
    A = are[gidx]
    Aim = aim[gidx]
    Ld = np.broadcast_to(ldt[gidx][..., None], A.shape)
    for j, src in enumerate([A, Aim, Ld]):
        w1[j] = np.broadcast_to(src[:, None, None, :, :, :], (4, 2, H, 16, 2, Pn))
    Bre = bre[gidx]
    Bim = bim[gidx]
    for g2 in range(2):
        w1[3][:, g2, :, :, g2, :] = Bre[:, :, g2, :, :].transpose(0, 3, 1, 2)
        w1[4][:, g2, :, :, g2, :] = Bim[:, :, g2, :, :].transpose(0, 3, 1, 2)
    shared["ssm_w1"] = np.ascontiguousarray(w1.reshape(5, 128, 16 * 128).transpose(1, 0, 2))
    w3 = np.zeros((2, 2, Pn, NPAIR, 2, H), np.float32)
    c4 = cre.reshape(NPAIR, 2, H, Pn)
    ci4 = cim.reshape(NPAIR, 2, H, Pn)
    for g2 in range(2):
        w3[0][g2, :, :, g2, :] = c4[:, g2].transpose(2, 0, 1)
        w3[1][g2, :, :, g2, :] = ci4[:, g2].transpose(2, 0, 1)
    shared["ssm_w3"] = np.ascontiguousarray(w3.reshape(2, 128, NPAIR * 32).transpose(1, 0, 2))
    shared["ssm_d"] = np.ascontiguousarray(f(ssm_d)[0].reshape(16, 128).T)
    shared["conv_w"] = np.ascontiguousarray(f(conv_w)[0].reshape(3, 16, 128).transpose(2, 0, 1))

    key = (nt_pre, nt_main)
    if key not in _CACHE:
        _CACHE[key] = build(cfg)
    nc = _CACHE[key]

    half = nt_main * T
    in_maps = []
    for c in range(n):
        b, hf = c // 2, c % 2
        m = dict(shared)
        m["x_main"] = _fm(x[b, hf * 2048: hf * 2048 + max(half, T)])
        m["x_pre"] = _fm(x[b, 0: max(nt_pre, 1) * T])
        m["memT"] = np.ascontiguousarray(mem[b].reshape(NMEM, KC, 128).transpose(1, 2, 0))
        m["maskb"] = np.full((128, 1), float(hf), np.float32)
        in_maps.append(m)
    res = run_bass_kernel_spmd(nc, in_maps, core_ids=list(range(n)))
    out = np.zeros((4, 4096, D), np.float32)
    for c in range(n):
        b, hf = c // 2, c % 2
        o = res.results[c]["out"]
        ntk = o.shape[0]
        tok = o.transpose(0, 3, 1, 2).reshape(ntk * T, D)
        out[b, hf * 2048: hf * 2048 + ntk * T] = tok
    return out
```

```python
import math
from contextlib import ExitStack
import numpy as np
import concourse.bass as bass
import concourse.mybir as mybir
from concourse.bass_utils import run_bass_kernel_spmd

F32 = mybir.dt.float32
BF16 = mybir.dt.bfloat16
I32 = mybir.dt.int32
ALU = mybir.AluOpType
AF = mybir.ActivationFunctionType
AX = mybir.AxisListType

D = 4096
KC = 32
T = 512
DFF = 11008
HC = 86
PARTS = [22, 22, 21, 21]
NMEM = 256
NPAIR = 64
EPS = 1e-6
TWO_PI = 2.0 * math.pi
NWB = 4
NHK = 4
NHN = 3
PF = 3

CFG = {"nt_pre": 4, "nt_main": 4}


class Buf:
    __slots__ = ("name", "lw", "rd", "rd_dma")

    def __init__(self, name):
        self.name = name
        self.lw = None
        self.rd = {}
        self.rd_dma = []


class Op:
    __slots__ = ("eng", "fn", "deps", "dkey", "needs_inc", "val", "idx")


ENGS = ("pe", "act", "dve", "pool", "sp")


class Prog:
    def __init__(self):
        self.ops = {e: [] for e in ENGS}
        self.dcount = {}
        self.nbuf = 0

    def buf(self, name="b"):
        self.nbuf += 1
        return Buf(name)

    def bufs(self, n, name="b"):
        return [self.buf(f"{name}{i}") for i in range(n)]

    def op(self, eng, fn, r=(), w=(), dkey=None):
        o = Op()
        o.eng = eng
        o.fn = fn
        o.dkey = dkey
        o.needs_inc = False
        o.val = None
        o.idx = len(self.ops[eng])
        deps = {}

        def add(d):
            if d is None or d is o:
                return
            if d.dkey is not None:
                deps[("d", id(d))] = d
            else:
                if d.eng == "pe" and eng == "pe":
                    return
                k = ("c", d.eng)
                if k not in deps or deps[k].idx < d.idx:
                    deps[k] = d

        for b in r:
            add(b.lw)
        for b in w:
            add(b.lw)
            for d in b.rd.values():
                add(d)
            for d in b.rd_dma:
                add(d)
        o.deps = list(deps.values())
        for d in o.deps:
            d.needs_inc = True
        if dkey is not None:
            self.dcount[dkey] = self.dcount.get(dkey, 0) + 16
            o.val = self.dcount[dkey]
        for b in r:
            if dkey is not None:
                b.rd_dma.append(o)
            else:
                b.rd[eng] = o
        for b in w:
            b.lw = o
            b.rd = {}
            b.rd_dma = []
        self.ops[eng].append(o)
        return o

    def emit(self, nc, es):
        csem = {e: es.enter_context(nc.semaphore("s_" + e)) for e in ("pe", "act", "dve", "pool")}
        dsem = {k: es.enter_context(nc.semaphore("d_" + k)) for k in self.dcount}
        for e in ("pe", "act", "dve", "pool"):
            c = 0
            for o in self.ops[e]:
                if o.needs_inc:
                    c += 1
                    o.val = c
        block = es.enter_context(nc.Block())
        prog = self

        def run(engname, eng):
            waited = {}
            for o in prog.ops[engname]:
                for d in o.deps:
                    if d.dkey is not None:
                        sem = dsem[d.dkey]
                        v = prog.dcount[d.dkey] if d.dkey == "init" else d.val
                        key = "d_" + d.dkey
                    else:
                        sem = csem[d.eng]
                        v = d.val
                        key = d.eng
                    if waited.get(key, 0) >= v:
                        continue
                    eng.wait_ge(sem, v)
                    waited[key] = v
                ins = o.fn(eng)
                if o.dkey is not None:
                    ins.then_inc(dsem[o.dkey], 16)
                elif o.needs_inc:
                    ins.then_inc(csem[engname], 1)
            if engname == "sp":
                for k, tot in prog.dcount.items():
                    eng.wait_ge(dsem[k], tot)

        @block.tensor
        def _(e):
            run("pe", e)

        @block.scalar
        def _(e):
            run("act", e)

        @block.vector
        def _(e):
            run("dve", e)

        @block.gpsimd
        def _(e):
            run("pool", e)

        @block.sync
        def _(e):
            run("sp", e)


def build(cfg):
    wseq = _build(cfg, None)
    return _build(cfg, wseq)


AHEAD = 3


def _build(cfg, WSEQ):
    dry = WSEQ is None
    REC = []
    nt_pre, nt_main = cfg["nt_pre"], cfg["nt_main"]
    nc = bass.Bass("TRN2", target_bir_lowering=False)
    P = Prog()
    es = ExitStack()

    def din(name, shape):
        return nc.dram_tensor(name, list(shape), F32, kind="ExternalInput").ap()

    x_main = din("x_main", [max(nt_main, 1), KC, 128, T])
    x_pre = din("x_pre", [max(nt_pre, 1), KC, 128, T])
    memT = din("memT", [KC, 128, NMEM])
    maskb = din("maskb", [128, 1])
    ident_d = din("ident", [128, 128])
    gains_d = din("gains", [128, 7, KC])
    WSCR = {}
    WAP = {}

    def dinw(name, shape):
        ap = din(name, shape)
        WAP[name] = ap
        WSCR[name] = nc.dram_tensor("scr_" + name, list(shape), BF16, kind="Internal").ap()
        return (name, ap)

    w_f1in = dinw("w_f1in", [2 * HC, 128, KC * 128])
    w_f1out = [dinw(f"w_f1out{i}", [KC, 128, PARTS[i] * 128]) for i in range(4)]
    w_f2in = dinw("w_f2in", [2 * HC, 128, KC * 128])
    w_f2out = [dinw(f"w_f2out{i}", [KC, 128, PARTS[i] * 128]) for i in range(4)]
    w_mixin = dinw("w_mixin", [128, 128, KC * 128])
    w_glu = dinw("w_glu", [64, 128, 16 * 128])
    w_cvout = dinw("w_cvout", [KC, 128, 16 * 128])
    w_mixout = dinw("w_mixout", [KC, 128, KC * 128])
    w_q = dinw("w_q", [KC, 128, KC * 128])
    w_k = dinw("w_k", [KC, 128, KC * 128])
    w_v = dinw("w_v", [KC, 128, KC * 128])
    w_o = dinw("w_o", [KC, 128, KC * 128])
    ssm_lane = din("ssm_lane", [128, 3, NPAIR])
    ssm_w1 = din("ssm_w1", [128, 5, 16 * 128])
    ssm_w3 = din("ssm_w3", [128, 2, NPAIR * 32])
    ssm_d = din("ssm_d", [128, 16])
    conv_w = din("conv_w", [128, 3, 16])
    out_d = nc.dram_tensor("out", [max(nt_main, 1), KC, 128, T], F32, kind="ExternalOutput").ap()
    hbuf = nc.dram_tensor("hbuf", [KC, 128, T], F32, kind="Internal").ap()
    tab_d = nc.dram_tensor("tab", [NPAIR, 128, 2 * T], F32, kind="Internal").ap()

    def sb(name, shape, dt=F32):
        return es.enter_context(nc.sbuf_tensor(name, list(shape), dt))

    xn = sb("xn", [128, KC, T], BF16)
    ar_a = sb("ar_a", [128, KC, T], BF16)
    ar_b = sb("ar_b", [128, 16, T], BF16)
    ar_c = sb("ar_c", [128, 16, T], BF16)
    tmp = sb("tmp", [128, 6, T], F32)
    srb = sb("srb", [128, 2, T], BF16)
    wst = [sb(f"wst{i}", [128, KC * 64], F32) for i in range(2)]
    wbf = [sb(f"wbf{i}", [128, KC * 128], BF16) for i in range(NWB)]
    hk = [sb(f"hk{i}", [128, T], F32) for i in range(NHK)]
    hn = [sb(f"hn{i}", [128, T], F32) for i in range(NHN)]
    sq = sb("sq", [128, T], F32)
    rstd = sb("rstd", [128, T], F32)
    tb = [sb(f"tb{i}", [128, 2 * T], F32) for i in range(2)]
    w1re = sb("w1re", [128, 16, 128], BF16)
    w1im = sb("w1im", [128, 16, 128], BF16)
    w3re = sb("w3re", [128, NPAIR, 32], BF16)
    w3im = sb("w3im", [128, NPAIR, 32], BF16)
    gains = sb("gains_s", [128, 7, KC])
    ones = sb("ones", [128, 128])
    ident = sb("ident_s", [128, 128])
    epst = sb("epst", [128, 1])
    mb = sb("mb", [128, 1])
    dsk = sb("dsk", [128, 16])
    cw = sb("cw", [128, 3, 16])
    S = sb("S", [128, NPAIR, 2])
    Rl = sb("Rl", [128, NPAIR])
    thl = sb("thl", [128, NPAIR])
    lane_in = sb("lane_in", [128, 3, NPAIR])
    halo = sb("halo", [128, 16, 2])
    cch = sb("cch", [128, T + 2])
    io = sq
    sm = sb("sm", [128, 8])
    psum = [es.enter_context(nc.psum_tensor(f"ps{i}", [128, T], F32)) for i in range(8)]

    B_xn = P.bufs(KC, "xn")
    B_a = P.bufs(KC, "ara")
    B_b = P.bufs(16, "arb")
    B_c = P.bufs(16, "arc")
    B_tmp = P.bufs(6, "tmp")
    B_srb = P.bufs(2, "srb")
    B_wst = P.bufs(2, "wst")
    B_wbf = P.bufs(NWB, "wbf")
    B_hk = P.bufs(NHK, "hk")
    B_hn = P.bufs(NHN, "hn")
    B_sq = P.buf("sq")
    B_rstd = P.buf("rstd")
    B_tbh = P.bufs(4, "tbh")
    B_tb = [[B_tbh[0], B_tbh[1]], [B_tbh[2], B_tbh[3]]]
    B_ps = P.bufs(8, "ps")
    B_const = P.buf("const")
    B_S = P.buf("S")
    B_halo = P.buf("halo")
    B_cch = P.buf("cch")
    B_sm = P.buf("sm")
    B_hbuf = P.bufs(KC, "hbuf")
    B_tab = P.bufs(NPAIR, "tab")
    B_w1 = P.buf("w1")
    B_w3 = P.buf("w3")

    st = {"w": 0, "in_ssm": False, "hkw": 0, "issued": 0, "ws": 0, "ps": 0, "hk": 0, "hn": 0, "cast": 0, "tb": 0}

    def next_ps():
        i = st["ps"] % (4 if st["in_ssm"] else 8)
        st["ps"] += 1
        return i

    B_init = P.buf("initscratch")
    B_pc = P.buf("poolconst")
    B_lds = []

    def ld_const(dst_ap, src_ap):
        b = P.buf("ld")
        B_lds.append(b)
        P.op("sp", lambda e, d=dst_ap, s=src_ap: e.dma_start(out=d, in_=s), w=[b], dkey="init")

    ld_const(gains[:], gains_d[:, :, :])
    ld_const(ident[:], ident_d[:, :])
    ld_const(mb[:], maskb[:, :])
    ld_const(dsk[:], ssm_d[:, :])
    ld_const(cw[:], conv_w[:, :, :])
    ld_const(lane_in[:], ssm_lane[:, :, :])
    wa = ar_a[:].rearrange("p a b -> p (a b)").bitcast(F32).rearrange("p (a b) -> p a b", a=4)
    wb_ = xn[:].rearrange("p a b -> p (a b)").bitcast(F32).rearrange("p (a b) -> p a b", a=4)
    wc = ar_b[:].rearrange("p a b -> p (a b)").bitcast(F32).rearrange("p (a b) -> p a b", a=2)
    wd = ar_c[:].rearrange("p a b -> p (a b)").bitcast(F32).rearrange("p (a b) -> p a b", a=2)
    for j in range(5):
        dst = wa[:, j, :] if j < 4 else wb_[:, 0, :]
        ld_const(dst, ssm_w1[:, j, :])
    P.op("pool", lambda e: e.memset(ones[:], 1.0), w=[B_pc])
    P.op("pool", lambda e: e.memset(epst[:], EPS), w=[B_pc])
    P.op("pool", lambda e: e.iota(io[:], pattern=[[1, T]], base=1, channel_multiplier=0,
                                  allow_small_or_imprecise_dtypes=True), w=[B_pc])
    P.op("pool", lambda e: e.memset(S[:], 0.0), w=[B_S])
    P.op("pool", lambda e: e.memset(halo[:], 0.0), w=[B_halo])

    IR = [B_init, B_pc] + B_lds

    def dve(fn, r=(), w=()):
        return P.op("dve", fn, r=list(r), w=list(w))

    def act(fn, r=(), w=()):
        return P.op("act", fn, r=list(r), w=list(w))

    def idve(fn, extra_w=()):
        return P.op("dve", fn, r=IR, w=[B_init] + list(extra_w))

    def iact(fn, extra_w=()):
        return P.op("act", fn, r=IR, w=[B_init] + list(extra_w))

    def range_reduce(x_ap, k_i32, k_f32):
        idve(lambda e: e.tensor_scalar(k_i32, x_ap, 1.0 / TWO_PI, None, ALU.mult))
        idve(lambda e: e.tensor_copy(k_f32, k_i32))
        idve(lambda e: e.scalar_tensor_tensor(x_ap, k_f32, -TWO_PI, x_ap, ALU.mult, ALU.add))
        idve(lambda e: e.tensor_scalar(k_f32, x_ap, math.pi, None, ALU.is_gt))
        idve(lambda e: e.scalar_tensor_tensor(x_ap, k_f32, -TWO_PI, x_ap, ALU.mult, ALU.add))
        idve(lambda e: e.tensor_scalar(k_f32, x_ap, -math.pi, None, ALU.is_lt))
        idve(lambda e: e.scalar_tensor_tensor(x_ap, k_f32, TWO_PI, x_ap, ALU.mult, ALU.add))

    dtl = tmp[:, 0, 0:NPAIR]
    ki_l = tmp[:, 1, 0:NPAIR].bitcast(I32)
    kf_l = tmp[:, 2, 0:NPAIR]
    iact(lambda e: e.activation(dtl, lane_in[:, 2, :], AF.Exp))
    idve(lambda e: e.tensor_tensor(Rl[:], lane_in[:, 0, :], dtl, ALU.mult))
    iact(lambda e: e.activation(Rl[:], Rl[:], AF.Exp))
    idve(lambda e: e.tensor_tensor(thl[:], lane_in[:, 1, :], dtl, ALU.mult))
    range_reduce(thl[:], ki_l, kf_l)

    for pi in range(NPAIR):
        slot = pi % 2
        tbs, Bt = tb[slot], B_tb[slot]
        ang = tmp[:, 0, :]
        ki = tmp[:, 1, :].bitcast(I32)
        kf = tmp[:, 2, :]
        ang2 = tmp[:, 3, :]
        idve(lambda e, pi=pi, ang=ang: e.tensor_scalar(ang, io[:], thl[:, pi:pi + 1], None, ALU.mult))
        idve(lambda e, ang=ang, ang2=ang2: e.tensor_scalar(ang2, ang, math.pi / 2, None, ALU.add))
        range_reduce(ang, ki, kf)
        iact(lambda e, tbs=tbs, ang=ang: e.activation(tbs[:, T:2 * T], ang, AF.Sin), extra_w=Bt)
        range_reduce(ang2, ki, kf)
        iact(lambda e, tbs=tbs, ang2=ang2: e.activation(tbs[:, 0:T], ang2, AF.Sin), extra_w=Bt)
        P.op("sp", lambda e, pi=pi, tbs=tbs: e.dma_start(out=tab_d[pi, :, :], in_=tbs[:]), r=Bt, w=[B_tab[pi]],
             dkey=f"tb{slot}")

    a_re, a_im, ldt, b_re, b_im = wa[:, 0, :], wa[:, 1, :], wa[:, 2, :], wa[:, 3, :], wb_[:, 0, :]
    t1, t2, t3 = wb_[:, 1, :], wb_[:, 2, :], wb_[:, 3, :]
    t4, t5, t6, t7 = wc[:, 0, :], wc[:, 1, :], wd[:, 0, :], wd[:, 1, :]
    iact(lambda e: e.activation(ldt, ldt, AF.Exp))
    idve(lambda e: e.tensor_tensor(t1, a_re, ldt, ALU.mult))
    iact(lambda e: e.activation(t1, t1, AF.Exp))
    idve(lambda e: e.tensor_tensor(t2, a_im, ldt, ALU.mult))
    idve(lambda e: e.tensor_scalar(t3, t2, math.pi / 2, None, ALU.add))
    range_reduce(t2, t6.bitcast(I32), t7)
    range_reduce(t3, t6.bitcast(I32), t7)
    iact(lambda e: e.activation(t2, t2, AF.Sin))
    iact(lambda e: e.activation(t3, t3, AF.Sin))
    idve(lambda e: e.tensor_tensor(t3, t3, t1, ALU.mult))
    idve(lambda e: e.tensor_tensor(t2, t2, t1, ALU.mult))
    idve(lambda e: e.tensor_scalar(t3, t3, -1.0, None, ALU.add))
    idve(lambda e: e.tensor_tensor(t1, a_re, a_re, ALU.mult))
    idve(lambda e: e.tensor_tensor(t4, a_im, a_im, ALU.mult))
    idve(lambda e: e.tensor_tensor(t1, t1, t4, ALU.add))
    idve(lambda e: e.reciprocal(t1, t1))
    idve(lambda e: e.tensor_tensor(t4, t3, a_re, ALU.mult))
    idve(lambda e: e.tensor_tensor(t5, t2, a_im, ALU.mult))
    idve(lambda e: e.tensor_tensor(t4, t4, t5, ALU.add))
    idve(lambda e: e.tensor_tensor(t4, t4, t1, ALU.mult))
    idve(lambda e: e.tensor_tensor(t5, t2, a_re, ALU.mult))
    idve(lambda e: e.tensor_tensor(t6, t3, a_im, ALU.mult))
    idve(lambda e: e.tensor_tensor(t5, t5, t6, ALU.subtract))
    idve(lambda e: e.tensor_tensor(t5, t5, t1, ALU.mult))
    idve(lambda e: e.tensor_tensor(t6, t4, b_re, ALU.mult))
    idve(lambda e: e.tensor_tensor(t7, t5, b_im, ALU.mult))
    idve(lambda e: e.tensor_tensor(w1re[:].rearrange("p a b -> p (a b)"), t6, t7, ALU.subtract), extra_w=[B_w1])
    idve(lambda e: e.tensor_tensor(t6, t4, b_im, ALU.mult))
    idve(lambda e: e.tensor_tensor(t7, t5, b_re, ALU.mult))
    idve(lambda e: e.tensor_tensor(w1im[:].rearrange("p a b -> p (a b)"), t6, t7, ALU.add), extra_w=[B_w1])
    P.op("sp", lambda e: e.dma_start(out=t6, in_=ssm_w3[:, 0, :]), r=[B_init], w=[B_init], dkey="i2a")
    P.op("sp", lambda e: e.dma_start(out=t7, in_=ssm_w3[:, 1, :]), r=[B_init], w=[B_init], dkey="i2b")
    idve(lambda e: e.tensor_copy(w3re[:].rearrange("p a b -> p (a b)"), t6), extra_w=[B_w3])
    idve(lambda e: e.tensor_scalar(w3im[:].rearrange("p a b -> p (a b)"), t7, -1.0, None, ALU.mult), extra_w=[B_w3])
    idve(lambda e: e.memset(sm[:, 7:8], 0.0), extra_w=[B_const, B_sq] + B_a + B_xn + B_b + B_c + B_tmp)

    SEEN = {}
    PEND = []

    def flush_store():
        i, n, scr, blk, b = PEND.pop(0)
        P.op("sp", lambda e, i=i, n=n, scr=scr, blk=blk: e.dma_start(out=scr[blk, :, :], in_=wbf[i][:, 0:n]),
             r=[B_wbf[i]], w=[b], dkey=f"wb{i}")

    def issue_load(k):
        wref, blk, ncols = WSEQ[k]
        wname = wref[0]
        wap = WAP[wname]
        key = (wname, blk)
        i = k % NWB
        n = ncols
        while len(PEND) > 2 or any(p[0] == i for p in PEND):
            flush_store()
        if key not in SEEN:
            nh = n // 2
            for j in range(2):
                P.op("sp", lambda e, j=j, nh=nh, wap=wap, blk=blk: e.dma_start(out=wst[j][:, 0:nh],
                                                                              in_=wap[blk, :, j * nh:(j + 1) * nh]),
                     w=[B_wst[j]], dkey=f"ws{j}")
                ce = "dve" if (st["cast"] % 3 == 2) else "act"
                st["cast"] += 1
                if ce == "act":
                    P.op("act", lambda e, i=i, j=j, nh=nh: e.activation(wbf[i][:, j * nh:(j + 1) * nh], wst[j][:, 0:nh], AF.Copy),
                         r=[B_wst[j]], w=[B_wbf[i]])
                else:
                    P.op("dve", lambda e, i=i, j=j, nh=nh: e.tensor_copy(wbf[i][:, j * nh:(j + 1) * nh], wst[j][:, 0:nh]),
                         r=[B_wst[j]], w=[B_wbf[i]])
            b = P.buf("scr")
            SEEN[key] = b
            PEND.append((i, n, WSCR[wname], blk, b))
        else:
            for pe_ in list(PEND):
                if pe_[4] is SEEN[key]:
                    while PEND:
                        flush_store()
            scr = WSCR[wname]
            P.op("sp", lambda e, i=i, n=n, scr=scr, blk=blk: e.dma_start(out=wbf[i][:, 0:n], in_=scr[blk, :, :]),
                 r=[SEEN[key]], w=[B_wbf[i]], dkey=f"wl{i}")

    def wload(wref, blk, ncols):
        k = st["w"]
        st["w"] += 1
        if dry:
            REC.append((wref, blk, ncols))
            return wbf[k % NWB], B_wbf[k % NWB]
        assert WSEQ[k][1] == blk and WSEQ[k][2] == ncols and WSEQ[k][0][0] == wref[0]
        while st["issued"] < min(k + AHEAD + 1, len(WSEQ)):
            issue_load(st["issued"])
            st["issued"] += 1
        return wbf[k % NWB], B_wbf[k % NWB]

    def mm_group(ps_i, wt, Bw, rhs_list, n=T, ps_cols=None):
        nk = len(rhs_list)
        pcols = ps_cols if ps_cols is not None else slice(0, n)
        for k, (rap, rb) in enumerate(rhs_list):
            P.op("pe", lambda e, k=k, rap=rap, ps_i=ps_i, wt=wt, nk=nk, pcols=pcols: e.matmul(
                psum[ps_i][:, pcols], wt[:, k * 128:(k + 1) * 128], rap, start=(k == 0), stop=(k == nk - 1)),
                r=[Bw, rb], w=[B_ps[ps_i]])

    tbh = [tb[0][:, 0:T], tb[0][:, T:2 * T], tb[1][:, 0:T], tb[1][:, T:2 * T]]

    def hk_load(src_ap, n=T, extra_r=(), wide=False):
        if wide:
            i = st["hkw"] % (NHK + 4)
            st["hkw"] += 1
        else:
            i = st["hk"] % NHK
            st["hk"] += 1
        if i < NHK:
            buf, Bb, key = hk[i], B_hk[i], f"hk{i}"
        else:
            buf, Bb, key = tbh[i - NHK], B_tbh[i - NHK], f"th{i - NHK}"
        P.op("sp", lambda e, buf=buf, s=src_ap, n=n: e.dma_start(out=buf[:, 0:n], in_=s), r=list(extra_r),
             w=[Bb], dkey=key)
        return buf, Bb

    def rms_stats(chunk_src, n, nchunks=KC):
        ps_i = next_ps()
        for k in range(nchunks):
            src, rb = chunk_src(k)
            hkt, Bh = hk_load(src, n, rb, wide=True)
            act(lambda e, hkt=hkt, n=n: e.activation(sq[:, 0:n], hkt[:, 0:n], AF.Square), r=[Bh], w=[B_sq])
            P.op("pe", lambda e, k=k, n=n, ps_i=ps_i, nchunks=nchunks: e.matmul(
                psum[ps_i][:, 0:n], ones[:, :], sq[:, 0:n], start=(k == 0), stop=(k == nchunks - 1)),
                r=[B_sq, B_const], w=[B_ps[ps_i]])
        act(lambda e, n=n, ps_i=ps_i: e.activation(rstd[:, 0:n], psum[ps_i][:, 0:n], AF.Sqrt, bias=epst[:, 0:1],
                                                     scale=1.0 / D), r=[B_ps[ps_i], B_const], w=[B_rstd])
        dve(lambda e, n=n: e.reciprocal(rstd[:, 0:n], rstd[:, 0:n]), r=[B_rstd], w=[B_rstd])

    def norm_from_hbuf(gi):
        rms_stats(lambda k: (hbuf[k, :, :], [B_hbuf[k]]), T)
        for k in range(KC):
            hkt, Bh = hk_load(hbuf[k, :, :], T, [B_hbuf[k]], wide=True)
            dve(lambda e, k=k, hkt=hkt: e.scalar_tensor_tensor(xn[:, k, :], hkt[:, :], gains[:, gi, k:k + 1], rstd[:, :],
                                                               ALU.mult, ALU.mult),
                r=[Bh, B_rstd, B_const], w=[B_xn[k]])

    HKQ = {}

    def ep_prefetch(m):
        if m < KC:
            HKQ[m] = hk_load(hbuf[m, :, :], T, [B_hbuf[m]])

    def ep_begin():
        for m in range(PF):
            ep_prefetch(m)

    def epilogue(ps_i, m, scale):
        ep_prefetch(m + PF)
        hkt, Bh = HKQ.pop(m)
        j = st["hn"] % NHN
        st["hn"] += 1
        dve(lambda e, hkt=hkt, j=j, ps_i=ps_i, scale=scale: e.scalar_tensor_tensor(
            hn[j][:, :], psum[ps_i][:, :], scale, hkt[:, :], ALU.mult, ALU.add),
            r=[B_ps[ps_i], Bh], w=[B_hn[j]])
        P.op("sp", lambda e, j=j, m=m: e.dma_start(out=hbuf[m, :, :], in_=hn[j][:, :]), r=[B_hn[j]], w=[B_hbuf[m]],
             dkey=f"hn{j}")

    xn_rhs = [(xn[:, k, :], B_xn[k]) for k in range(KC)]

    def ffn(w_in, w_out, gi):
        norm_from_hbuf(gi)
        hc0 = 0
        for part, npart in enumerate(PARTS):
            for jj in range(npart):
                j = hc0 + jj
                wa_t, Bwa = wload(w_in, j, KC * 128)
                pa = next_ps()
                mm_group(pa, wa_t, Bwa, xn_rhs)
                wb_t, Bwb = wload(w_in, HC + j, KC * 128)
                pb = next_ps()
                mm_group(pb, wb_t, Bwb, xn_rhs)
                ti = jj % 2
                act(lambda e, pa=pa, ti=ti: e.activation(tmp[:, ti, :], psum[pa][:, :], AF.Silu),
                    r=[B_ps[pa]], w=[B_tmp[ti]])
                dve(lambda e, pb=pb, ti=ti, jj=jj: e.tensor_tensor(ar_a[:, jj, :], tmp[:, ti, :], psum[pb][:, :], ALU.mult),
                    r=[B_ps[pb], B_tmp[ti]], w=[B_a[jj]])
            g_rhs = [(ar_a[:, jj, :], B_a[jj]) for jj in range(npart)]
            ep_begin()
            for m in range(KC):
                wo_t, Bwo = wload(w_out[part], m, npart * 128)
                po = next_ps()
                mm_group(po, wo_t, Bwo, g_rhs)
                epilogue(po, m, 0.5)
            hc0 += npart

    def ssm(full, B_u):
        st["in_ssm"] = True
        for c in range(16):
            yps = next_ps() if full else None
            for q in range(4):
                pi = 4 * c + q
                par = pi % 2
                pA, pB = 4 + 2 * par, 5 + 2 * par
                rows = slice(32 * q, 32 * q + 32)
                P.op("pe", lambda e, c=c, q=q, rows=rows, pA=pA: e.matmul(
                    psum[pA][:, :], w1re[rows, c, :], ar_a[rows, c, :], start=True, stop=True, tile_position=(32 * q, 0)),
                    r=[B_w1, B_u[c]], w=[B_ps[pA]])
                P.op("pe", lambda e, c=c, q=q, rows=rows, pB=pB: e.matmul(
                    psum[pB][:, :], w1im[rows, c, :], ar_a[rows, c, :], start=True, stop=True, tile_position=(32 * q, 0)),
                    r=[B_w1, B_u[c]], w=[B_ps[pB]])
                ts = st["tb"] % 2
                st["tb"] += 1
                P.op("sp", lambda e, ts=ts, pi=pi: e.dma_start(out=tb[ts][:], in_=tab_d[pi, :, :]), r=[B_tab[pi]],
                     w=B_tb[ts], dkey=f"tb{ts}")
                cs, sn = tb[ts][:, 0:T], tb[ts][:, T:2 * T]
                Bt = B_tb[ts]
                A_, B_ = psum[pA][:, :], psum[pB][:, :]
                t = [tmp[:, i, :] for i in range(6)]
                TT = lambda o, a, b, op, r, w: dve(lambda e, o=o, a=a, b=b, op=op: e.tensor_tensor(o, a, b, op), r=r, w=w)
                TT(t[0], A_, cs, ALU.mult, [B_ps[pA]] + Bt, [B_tmp[0]])
                TT(t[1], B_, sn, ALU.mult, [B_ps[pB]] + Bt, [B_tmp[1]])
                TT(t[0], t[0], t[1], ALU.add, [B_tmp[0], B_tmp[1]], [B_tmp[0]])
                TT(t[2], B_, cs, ALU.mult, [B_ps[pB]] + Bt, [B_tmp[2]])
                TT(t[3], A_, sn, ALU.mult, [B_ps[pA]] + Bt, [B_tmp[3]])
                TT(t[2], t[2], t[3], ALU.subtract, [B_tmp[2], B_tmp[3]], [B_tmp[2]])
                rbc = Rl[:, pi:pi + 1].to_broadcast([128, T])
                dve(lambda e, pi=pi, rbc=rbc, o=t[1], i_=t[0]: e.tensor_tensor_scan(o, rbc, i_, S[:, pi, 0:1], ALU.mult, ALU.add),
                    r=[B_tmp[0], B_S, B_init], w=[B_tmp[1]])
                dve(lambda e, pi=pi, rbc=rbc, o=t[3], i_=t[2]: e.tensor_tensor_scan(o, rbc, i_, S[:, pi, 1:2], ALU.mult, ALU.add),
                    r=[B_tmp[2], B_S, B_init], w=[B_tmp[3]])
                n0 = 0 if full else T - 1
                sl = slice(n0, T)
                TT(t[0][:, sl], tb[ts][:, n0:T], t[1][:, sl], ALU.mult, [B_tmp[1]] + Bt, [B_tmp[0]])
                TT(t[2][:, sl], tb[ts][:, T + n0:2 * T], t[3][:, sl], ALU.mult, [B_tmp[3]] + Bt, [B_tmp[2]])
                TT(t[4][:, sl], t[0][:, sl], t[2][:, sl], ALU.subtract, [B_tmp[0], B_tmp[2]], [B_tmp[4]])
                TT(t[0][:, sl], tb[ts][:, T + n0:2 * T], t[1][:, sl], ALU.mult, [B_tmp[1]] + Bt, [B_tmp[0]])
                TT(t[2][:, sl], tb[ts][:, n0:T], t[3][:, sl], ALU.mult, [B_tmp[3]] + Bt, [B_tmp[2]])
                TT(t[5][:, sl], t[0][:, sl], t[2][:, sl], ALU.add, [B_tmp[0], B_tmp[2]], [B_tmp[5]])
                dve(lambda e, pi=pi: e.tensor_copy(S[:, pi, 0:1], tmp[:, 4, T - 1:T]), r=[B_tmp[4]], w=[B_S])
                dve(lambda e, pi=pi: e.tensor_copy(S[:, pi, 1:2], tmp[:, 5, T - 1:T]), r=[B_tmp[5]], w=[B_S])
                if full:
                    act(lambda e: e.activation(srb[:, 0, :], tmp[:, 4, :], AF.Copy), r=[B_tmp[4]], w=[B_srb[0]])
                    act(lambda e: e.activation(srb[:, 1, :], tmp[:, 5, :], AF.Copy), r=[B_tmp[5]], w=[B_srb[1]])
                    P.op("pe", lambda e, pi=pi, rows=rows, yps=yps, q=q: e.matmul(
                        psum[yps][rows, :], w3re[:, pi, :], srb[:, 0, :], start=True, stop=False, tile_position=(0, 32 * q)),
                        r=[B_w3, B_srb[0]], w=[B_ps[yps]])
                    P.op("pe", lambda e, pi=pi, rows=rows, yps=yps, q=q: e.matmul(
                        psum[yps][rows, :], w3im[:, pi, :], srb[:, 1, :], start=False, stop=True, tile_position=(0, 32 * q)),
                        r=[B_w3, B_srb[1]], w=[B_ps[yps]])
            if full:
                dve(lambda e, c=c, yps=yps: e.scalar_tensor_tensor(tmp[:, 0, :], ar_a[:, c, :], dsk[:, c:c + 1], psum[yps][:, :],
                                                                   ALU.mult, ALU.add),
                    r=[B_u[c], B_ps[yps], B_const], w=[B_tmp[0]])
                act(lambda e, c=c: e.activation(ar_c[:, c, :], tmp[:, 0, :], AF.Gelu), r=[B_tmp[0]], w=[B_c[c]])
        st["in_ssm"] = False

    def conv_branch(full):
        for c in range(16):
            wt, Bw = wload(w_mixin, 32 + c, KC * 128)
            pcc = next_ps()
            mm_group(pcc, wt, Bw, xn_rhs)
            act(lambda e, pcc=pcc: e.activation(tmp[:, 0, :], psum[pcc][:, :], AF.Copy), r=[B_ps[pcc]], w=[B_tmp[0]])
            wt, Bw = wload(w_mixin, 48 + c, KC * 128)
            pch = next_ps()
            mm_group(pch, wt, Bw, xn_rhs)
            dve(lambda e, c=c: e.tensor_copy(cch[:, 0:2], halo[:, c, :]), r=[B_halo], w=[B_cch])
            dve(lambda e, pch=pch: e.tensor_tensor(cch[:, 2:T + 2], tmp[:, 0, :], psum[pch][:, :], ALU.mult),
                r=[B_tmp[0], B_ps[pch]], w=[B_cch])
            dve(lambda e, c=c: e.tensor_copy(halo[:, c, :], cch[:, T:T + 2]), r=[B_cch], w=[B_halo])
            if not full:
                continue
            wt, Bw = wload(w_mixin, 16 + c, KC * 128)
            pcb = next_ps()
            mm_group(pcb, wt, Bw, xn_rhs)
            dve(lambda e, c=c: e.tensor_scalar(tmp[:, 1, :], cch[:, 0:T], cw[:, 0, c:c + 1], None, ALU.mult),
                r=[B_cch, B_const], w=[B_tmp[1]])
            dve(lambda e, c=c: e.scalar_tensor_tensor(tmp[:, 1, :], cch[:, 1:T + 1], cw[:, 1, c:c + 1], tmp[:, 1, :],
                                                      ALU.mult, ALU.add), r=[B_cch, B_const, B_tmp[1]], w=[B_tmp[1]])
            dve(lambda e, c=c: e.scalar_tensor_tensor(tmp[:, 1, :], cch[:, 2:T + 2], cw[:, 2, c:c + 1], tmp[:, 1, :],
                                                      ALU.mult, ALU.add), r=[B_cch, B_const, B_tmp[1]], w=[B_tmp[1]])
            dve(lambda e, c=c, pcb=pcb: e.tensor_tensor(ar_b[:, c, :], tmp[:, 1, :], psum[pcb][:, :], ALU.mult),
                r=[B_tmp[1], B_ps[pcb]], w=[B_b[c]])

    def u_proj():
        for c in range(16):
            wt, Bw = wload(w_mixin, c, KC * 128)
            pu = next_ps()
            mm_group(pu, wt, Bw, xn_rhs)
            act(lambda e, c=c, pu=pu: e.activation(ar_a[:, c, :], psum[pu][:, :], AF.Copy), r=[B_ps[pu]], w=[B_a[c]])

    def mixer():
        norm_from_hbuf(1)
        u_proj()
        conv_branch(True)
        ssm(True, B_a)
        ys_rhs = [(ar_c[:, c, :], B_c[c]) for c in range(16)]
        cv_rhs = [(ar_b[:, c, :], B_b[c]) for c in range(16)]
        for m in range(KC):
            wt, Bw = wload(w_mixin, 64 + m, KC * 128)
            p1 = next_ps()
            mm_group(p1, wt, Bw, xn_rhs)
            act(lambda e, p1=p1: e.activation(tmp[:, 0, :], psum[p1][:, :], AF.Sigmoid), r=[B_ps[p1]], w=[B_tmp[0]])
            wt, Bw = wload(w_glu, m, 16 * 128)
            p2 = next_ps()
            mm_group(p2, wt, Bw, ys_rhs)
            dve(lambda e, p2=p2: e.tensor_tensor(tmp[:, 1, :], tmp[:, 0, :], psum[p2][:, :], ALU.mult),
                r=[B_tmp[0], B_ps[p2]], w=[B_tmp[1]])
            wt, Bw = wload(w_glu, 32 + m, 16 * 128)
            p3 = next_ps()
            mm_group(p3, wt, Bw, ys_rhs)
            act(lambda e, p3=p3: e.activation(tmp[:, 2, :], psum[p3][:, :], AF.Sigmoid), r=[B_ps[p3]], w=[B_tmp[2]])
            dve(lambda e: e.tensor_tensor(tmp[:, 1, :], tmp[:, 1, :], tmp[:, 2, :], ALU.mult),
                r=[B_tmp[1], B_tmp[2]], w=[B_tmp[1]])
            wt, Bw = wload(w_mixin, 96 + m, KC * 128)
            p4 = next_ps()
            mm_group(p4, wt, Bw, xn_rhs)
            act(lambda e, p4=p4: e.activation(tmp[:, 3, :], psum[p4][:, :], AF.Sigmoid), r=[B_ps[p4]], w=[B_tmp[3]])
            wt, Bw = wload(w_cvout, m, 16 * 128)
            p5 = next_ps()
            mm_group(p5, wt, Bw, cv_rhs)
            dve(lambda e, p5=p5: e.tensor_tensor(tmp[:, 4, :], tmp[:, 3, :], psum[p5][:, :], ALU.mult),
                r=[B_tmp[3], B_ps[p5]], w=[B_tmp[4]])
            dve(lambda e, m=m: e.tensor_tensor(ar_a[:, m, :], tmp[:, 1, :], tmp[:, 4, :], ALU.add),
                r=[B_tmp[1], B_tmp[4]], w=[B_a[m]])
        mg_rhs = [(ar_a[:, k, :], B_a[k]) for k in range(KC)]
        ep_begin()
        for m in range(KC):
            wt, Bw = wload(w_mixout, m, KC * 128)
            po = next_ps()
            mm_group(po, wt, Bw, mg_rhs)
            epilogue(po, m, 1.0)

    def xattn():
        norm_from_hbuf(2)
        memn = ar_b[:].rearrange("p a b -> p (a b)").rearrange("p (a b) -> p a b", a=KC)
        rms_stats(lambda k: (memT[k, :, :], []), NMEM)
        for k in range(KC):
            hkt, Bh = hk_load(memT[k, :, :], NMEM, (), wide=True)
            dve(lambda e, k=k, hkt=hkt: e.scalar_tensor_tensor(memn[:, k, :], hkt[:, 0:NMEM], gains[:, 3, k:k + 1],
                                                               rstd[:, 0:NMEM], ALU.mult, ALU.mult),
                r=[Bh, B_rstd, B_const], w=[B_b[k // 2]])
        mem_rhs = [(memn[:, k, :], B_b[k // 2]) for k in range(KC)]
        arc_flat = ar_c[:].rearrange("p a b -> p (a b)")
        KTh = arc_flat[:, 0:2048].rearrange("p (a b) -> p a b", a=8)
        Vh = arc_flat[:, 2048:4096].rearrange("p (a b) -> p a b", a=2)
        qTh = arc_flat[:, 4096:8192].rearrange("p (a b) -> p a b", a=8)
        B_KT, B_V, B_q = B_c[0], B_c[1], B_c[2]
        for h in range(4):
            for dc in range(8):
                wt, Bw = wload(w_k, h * 8 + dc, KC * 128)
                pk = next_ps()
                mm_group(pk, wt, Bw, mem_rhs, n=NMEM)
                act(lambda e, dc=dc, pk=pk: e.activation(KTh[:, dc, :], psum[pk][:, 0:NMEM], AF.Copy), r=[B_ps[pk]], w=[B_KT])
            for dc in range(8):
                wt, Bw = wload(w_v, h * 8 + dc, KC * 128)
                pv = next_ps()
                for mc in range(2):
                    for k in range(KC):
                        P.op("pe", lambda e, k=k, mc=mc, pv=pv, wt=wt: e.matmul(
                            psum[pv][:, mc * 128:(mc + 1) * 128], memn[:, k, mc * 128:(mc + 1) * 128],
                            wt[:, k * 128:(k + 1) * 128], start=(k == 0), stop=(k == KC - 1)),
                            r=[Bw, B_b[k // 2]], w=[B_ps[pv]])
                for mc in range(2):
                    act(lambda e, dc=dc, mc=mc, pv=pv: e.activation(Vh[:, mc, dc * 128:(dc + 1) * 128],
                                                                     psum[pv][:, mc * 128:(mc + 1) * 128], AF.Copy),
                        r=[B_ps[pv]], w=[B_V])
            for dc in range(8):
                wt, Bw = wload(w_q, h * 8 + dc, KC * 128)
                pq = next_ps()
                mm_group(pq, wt, Bw, xn_rhs)
                act(lambda e, dc=dc, pq=pq: e.activation(qTh[:, dc, :], psum[pq][:, :], AF.Copy, scale=1.0 / 32.0),
                    r=[B_ps[pq]], w=[B_q])
            for tc in range(4):
                pss = next_ps()
                for dc in range(8):
                    P.op("pe", lambda e, dc=dc, tc=tc, pss=pss: e.matmul(
                        psum[pss][:, 0:NMEM], qTh[:, dc, tc * 128:(tc + 1) * 128], KTh[:, dc, :],
                        start=(dc == 0), stop=(dc == 7)), r=[B_q, B_KT], w=[B_ps[pss]])
                dve(lambda e, pss=pss: e.reduce_max(sm[:, 0:1], psum[pss][:, 0:NMEM], AX.X), r=[B_ps[pss]], w=[B_sm])
                dve(lambda e: e.tensor_scalar(sm[:, 1:2], sm[:, 0:1], -1.0, None, ALU.mult), r=[B_sm], w=[B_sm])
                act(lambda e, pss=pss: e.activation(tmp[:, 0, 0:NMEM], psum[pss][:, 0:NMEM], AF.Exp, bias=sm[:, 1:2],
                                                     accum_out=sm[:, 2:3]), r=[B_ps[pss], B_sm], w=[B_tmp[0], B_sm])
                dve(lambda e: e.reciprocal(sm[:, 3:4], sm[:, 2:3]), r=[B_sm], w=[B_sm])
                dve(lambda e: e.tensor_scalar(tmp[:, 1, 0:NMEM], tmp[:, 0, 0:NMEM], sm[:, 3:4], None, ALU.mult),
                    r=[B_tmp[0], B_sm], w=[B_tmp[1]])
                for mc in range(2):
                    ppt = next_ps()
                    P.op("pe", lambda e, mc=mc, ppt=ppt: e.matmul(psum[ppt][:, 0:128], tmp[:, 1, mc * 128:(mc + 1) * 128],
                                                                  ident[:, :], start=True, stop=True),
                         r=[B_tmp[1], B_const], w=[B_ps[ppt]])
                    act(lambda e, mc=mc, tc=tc, ppt=ppt: e.activation(srb[:, mc, tc * 128:(tc + 1) * 128], psum[ppt][:, 0:128],
                                                                       AF.Copy), r=[B_ps[ppt]], w=[B_srb[mc]])
            for dc in range(8):
                po_ = next_ps()
                for mc in range(2):
                    P.op("pe", lambda e, dc=dc, mc=mc, po_=po_: e.matmul(
                        psum[po_][:, :], Vh[:, mc, dc * 128:(dc + 1) * 128], srb[:, mc, :], start=(mc == 0), stop=(mc == 1)),
                        r=[B_V, B_srb[mc]], w=[B_ps[po_]])
                act(lambda e, dc=dc, h=h, po_=po_: e.activation(ar_a[:, h * 8 + dc, :], psum[po_][:, :], AF.Copy),
                    r=[B_ps[po_]], w=[B_a[h * 8 + dc]])
        o_rhs = [(ar_a[:, k, :], B_a[k]) for k in range(KC)]
        ep_begin()
        for m in range(KC):
            wt, Bw = wload(w_o, m, KC * 128)
            po = next_ps()
            mm_group(po, wt, Bw, o_rhs)
            epilogue(po, m, 1.0)

    def final_norm(ti):
        rms_stats(lambda k: (hbuf[k, :, :], [B_hbuf[k]]), T)
        for k in range(KC):
            hkt, Bh = hk_load(hbuf[k, :, :], T, [B_hbuf[k]], wide=True)
            j = st["hn"] % NHN
            st["hn"] += 1
            dve(lambda e, k=k, hkt=hkt, j=j: e.scalar_tensor_tensor(hn[j][:, :], hkt[:, :], gains[:, 5, k:k + 1], rstd[:, :],
                                                                    ALU.mult, ALU.mult),
                r=[Bh, B_rstd, B_const], w=[B_hn[j]])
            P.op("sp", lambda e, j=j, k=k, ti=ti: e.dma_start(out=out_d[ti, k, :, :], in_=hn[j][:, :]), r=[B_hn[j]],
                 dkey=f"hn{j}")

    def load_h(src):
        P.op("sp", lambda e, src=src: e.dma_start(out=hbuf[:, :, :], in_=src), w=B_hbuf, dkey="hcopy")

    for ti in range(nt_pre):
        load_h(x_pre[ti, :, :, :])
        ffn(w_f1in, w_f1out, 0)
        norm_from_hbuf(1)
        u_proj()
        if ti == nt_pre - 1:
            conv_branch(False)
        ssm(False, B_a)
    if nt_pre > 0:
        dve(lambda e: e.tensor_scalar(S[:].rearrange("p a b -> p (a b)"), S[:].rearrange("p a b -> p (a b)"), mb[:, 0:1],
                                      None, ALU.mult), r=[B_S, B_const], w=[B_S])
        dve(lambda e: e.tensor_scalar(halo[:].rearrange("p a b -> p (a b)"), halo[:].rearrange("p a b -> p (a b)"),
                                      mb[:, 0:1], None, ALU.mult), r=[B_halo, B_const], w=[B_halo])
    for ti in range(nt_main):
        load_h(x_main[ti, :, :, :])
        ffn(w_f1in, w_f1out, 0)
        mixer()
        xattn()
        ffn(w_f2in, w_f2out, 4)
        final_norm(ti)

    while PEND:
        flush_store()
    if dry:
        es.close()
        return REC
    P.emit(nc, es)
    es.close()
    return nc


def _blk(Wm, kparts=None):
    K, N = Wm.shape
    kc, nm = K // 128, N // 128
    return np.ascontiguousarray(Wm.reshape(kc, 128, nm, 128).transpose(2, 1, 0, 3).reshape(nm, 128, kc * 128))


def _fm(xtok):
    nt = xtok.shape[0] // T
    return np.ascontiguousarray(xtok.reshape(nt, T, KC, 128).transpose(0, 2, 3, 1))


_CACHE = {}


def kernel(x, mem, ffn1_norm, ffn1_w_in, ffn1_w_out, mix_norm, mix_w_in,
           ssm_a_re, ssm_a_im, ssm_log_dt, ssm_b_re, ssm_b_im, ssm_c_re, ssm_c_im,
           ssm_d, ssm_glu_w, conv_w, conv_w_out, mix_w_out,
           xattn_norm, mem_norm, xattn_wq, xattn_wk, xattn_wv, xattn_wo,
           ffn2_norm, ffn2_w_in, ffn2_w_out, final_norm, _cfg=None):
    cfg = dict(CFG if _cfg is None else _cfg)
    nt_pre, nt_main = cfg["nt_pre"], cfg["nt_main"]
    f = lambda a: np.asarray(a, dtype=np.float32)
    x, mem = f(x), f(mem)
    n = 8
    shared = {}
    shared["w_f1in"] = _blk(f(ffn1_w_in)[0])
    shared["w_f2in"] = _blk(f(ffn2_w_in)[0])
    k0 = 0
    for i, npart in enumerate(PARTS):
        shared[f"w_f1out{i}"] = _blk(f(ffn1_w_out)[0][k0 * 128:(k0 + npart) * 128])
        shared[f"w_f2out{i}"] = _blk(f(ffn2_w_out)[0][k0 * 128:(k0 + npart) * 128])
        k0 += npart
    shared["w_mixin"] = _blk(f(mix_w_in)[0])
    shared["w_glu"] = _blk(f(ssm_glu_w)[0])
    shared["w_cvout"] = _blk(f(conv_w_out)[0])
    shared["w_mixout"] = _blk(f(mix_w_out)[0])
    shared["w_q"] = _blk(f(xattn_wq)[0])
    shared["w_k"] = _blk(f(xattn_wk)[0])
    shared["w_v"] = _blk(f(xattn_wv)[0])
    shared["w_o"] = _blk(f(xattn_wo)[0])
    shared["ident"] = np.eye(128, dtype=np.float32)
    g = np.zeros((128, 7, KC), np.float32)
    for i, gv in enumerate([ffn1_norm, mix_norm, xattn_norm, mem_norm, ffn2_norm, final_norm]):
        g[:, i, :] = f(gv).reshape(KC, 128).T
    shared["gains"] = g
    are, aim, ldt = f(ssm_a_re)[0], f(ssm_a_im)[0], f(ssm_log_dt)[0]
    bre, bim = f(ssm_b_re)[0], f(ssm_b_im)[0]
    cre, cim = f(ssm_c_re)[0], f(ssm_c_im)[0]
    G, Pn, H = 128, 64, 16
    lane = np.zeros((128, 3, NPAIR), np.float32)
    ar4 = are.reshape(NPAIR, 2, Pn)
    lane[:, 0, :] = ar4.transpose(1, 2, 0).reshape(128, NPAIR)
    lane[:, 1, :] = aim.reshape(NPAIR, 2, Pn).transpose(1, 2, 0).reshape(128, NPAIR)
    lane[:, 2, :] = np.broadcast_to(ldt.reshape(NPAIR, 2, 1), (NPAIR, 2, Pn)).transpose(1, 2, 0).reshape(128, NPAIR)
    shared["ssm_lane"] = lane
    w1 = np.zeros((5, 4, 2, H, 16, 2, Pn), np.float32)
    gidx = (8 * np.arange(16)[None, :, None] + 2 * np.arange(4)[:, None, None] + np.arange(2)[None, None, :])
    A = are[gidx]
    Aim = aim[gidx]
    Ld = np.broadcast_to(ldt[gidx][..., None], A.shape)
    for j, src in enumerate([A, Aim, Ld]):
        w1[j] = np.broadcast_to(src[:, None, None, :, :, :], (4, 2, H, 16, 2, Pn))
    Bre = bre[gidx]
    Bim = bim[gidx]
    for g2 in range(2):
        w1[3][:, g2, :, :, g2, :] = Bre[:, :, g2, :, :].transpose(0, 3, 1, 2)
        w1[4][:, g2, :, :, g2, :] = Bim[:, :, g2, :, :].transpose(0, 3, 1, 2)
    shared["ssm_w1"] = np.ascontiguousarray(w1.reshape(5, 128, 16 * 128).transpose(1, 0, 2))
    w3 = np.zeros((2, 2, Pn, NPAIR, 2, H), np.float32)
    c4 = cre.reshape(NPAIR, 2, H, Pn)
    ci4 = cim.reshape(NPAIR, 2, H, Pn)
    for g2 in range(2):
        w3[0][g2, :, :, g2, :] = c4[:, g2].transpose(2, 0, 1)
        w3[1][g2, :, :, g2, :] = ci4[:, g2].transpose(2, 0, 1)
    shared["ssm_w3"] = np.ascontiguousarray(w3.reshape(2, 128, NPAIR * 32).transpose(1, 0, 2))
    shared["ssm_d"] = np.ascontiguousarray(f(ssm_d)[0].reshape(16, 128).T)
    shared["conv_w"] = np.ascontiguousarray(f(conv_w)[0].reshape(3, 16, 128).transpose(2, 0, 1))

    key = (nt_pre, nt_main)
    if key not in _CACHE:
        _CACHE[key] = build(cfg)
    nc = _CACHE[key]

    half = nt_main * T
    in_maps = []
    for c in range(n):
        b, hf = c // 2, c % 2
        m = dict(shared)
        m["x_main"] = _fm(x[b, hf * 2048: hf * 2048 + max(half, T)])
        m["x_pre"] = _fm(x[b, 0: max(nt_pre, 1) * T])
        m["memT"] = np.ascontiguousarray(mem[b].reshape(NMEM, KC, 128).transpose(1, 2, 0))
        m["maskb"] = np.full((128, 1), float(hf), np.float32)
        in_maps.append(m)
    res = run_bass_kernel_spmd(nc, in_maps, core_ids=list(range(n)))
    out = np.zeros((4, 4096, D), np.float32)
    for c in range(n):
        b, hf = c // 2, c % 2
        o = res.results[c]["out"]
        ntk = o.shape[0]
        tok = o.transpose(0, 3, 1, 2).reshape(ntk * T, D)
        out[b, hf * 2048: hf * 2048 + ntk * T] = tok
    return out
```
